# Optimizing a Trainium2 kernel written in Bass

```python
import math
import jax, jax.numpy as jnp
from jax import lax
import numpy as np

D_MODEL = 1024
BATCH = 8
SEQ = 2048
DEPTH = 1

PLE_DIM = 256
EPS = 1e-6
N_HEADS_MLA = 8
QK_NOPE_DIM = 64
QK_ROPE_DIM = 32
QK_HEAD_DIM = QK_NOPE_DIM + QK_ROPE_DIM
V_HEAD_DIM = 64
Q_LORA_RANK = 768
KV_LORA_RANK = 256
MLA_WIDTH = N_HEADS_MLA * V_HEAD_DIM
ROPE_BASE = 10000.0
Q_BLOCK = 128
HYENA_WIDTH = 512
FILTER_EMB_DIM = 33
FILTER_ORDER = 64
DECAY_TARGET = 1e-2
FAST_DECAY_PCT = 0.3
SLOW_DECAY_PCT = 1.5
MIX_WIDTH = MLA_WIDTH + HYENA_WIDTH
IN_PROJ_WIDTH = Q_LORA_RANK + KV_LORA_RANK + QK_ROPE_DIM + 3 * HYENA_WIDTH
D_FF = 2816

kernel_name = "hybrid_mla_hyena_convffn_ple_block"


def _rmsnorm(x, g):
    xf = x.astype(jnp.float32)
    y = xf * lax.rsqrt(jnp.mean(xf * xf, axis=-1, keepdims=True) + EPS)
    return (y * g.astype(jnp.float32)).astype(x.dtype)


def _dwconv3(x, w, b):
    xp = jnp.pad(x, ((0, 0), (1, 1), (0, 0)))
    return xp[:, :-2] * w[0] + xp[:, 1:-1] * w[1] + xp[:, 2:] * w[2] + b


def _rope(x, cos, sin):
    x1, x2 = jnp.split(x.astype(jnp.float32), 2, axis=-1)
    return jnp.concatenate([x1 * cos - x2 * sin, x2 * cos + x1 * sin], axis=-1).astype(x.dtype)


def _block_attention(q, k, v):
    B, S, H, Dq = q.shape
    nb = S // Q_BLOCK
    qb = q.reshape(B, nb, Q_BLOCK, H, Dq).transpose(1, 0, 2, 3, 4)
    scale = Dq ** -0.5

    def one_block(qi):
        s = jnp.einsum('bqhd,bkhd->bhqk', qi, k, preferred_element_type=jnp.float32) * scale
        w = jax.nn.softmax(s, axis=-1).astype(v.dtype)
        return jnp.einsum('bhqk,bkhd->bqhd', w, v)

    o = lax.map(one_block, qb)
    return o.transpose(1, 0, 2, 3, 4).reshape(B, S, H, v.shape[-1])


def _hyena_filters(L, w1, b1, w2, b2, w3, b3, w_out, freq):
    f32 = jnp.float32
    t = jnp.linspace(0.0, 1.0, L, dtype=f32)[:, None]
    bands = (FILTER_EMB_DIM - 1) // 2
    fb = jnp.linspace(1e-4, bands - 1, bands, dtype=f32)
    w = 2.0 * math.pi * jnp.arange(L, dtype=f32) / L
    ang = w[:, None] * fb[None, :]
    z = jnp.concatenate([t, jnp.cos(ang), -jnp.sin(ang)], axis=-1)
    fr = freq.astype(f32)
    h = jnp.sin(fr * (z @ w1.astype(f32) + b1.astype(f32)))
    h = jnp.sin(fr * (h @ w2.astype(f32) + b2.astype(f32)))
    h = jnp.sin(fr * (h @ w3.astype(f32) + b3.astype(f32)))
    h = h @ w_out.astype(f32)
    min_decay = math.log(DECAY_TARGET) / FAST_DECAY_PCT
    max_decay = math.log(DECAY_TARGET) / SLOW_DECAY_PCT
    deltas = jnp.abs(jnp.linspace(min_decay, max_decay, HYENA_WIDTH, dtype=f32))
    decay = jnp.exp(-t * deltas[None, :])
    h = h.reshape(L, 2, HYENA_WIDTH) * decay[:, None, :]
    return h[:, 0], h[:, 1]


def _bidir_long_conv(u, h_fwd, h_bwd, bias):
    B, L, C = u.shape
    n = 2 * L
    kcirc = jnp.concatenate([h_fwd, jnp.zeros((1, C), jnp.float32), h_bwd[1:][::-1]], axis=0)
    uf32 = u.astype(jnp.float32)
    uf = jnp.fft.rfft(uf32, n=n, axis=1)
    kf = jnp.fft.rfft(kcirc, n=n, axis=0)
    y = jnp.fft.irfft(uf * kf[None], n=n, axis=1)[:, :L]
    return (y + uf32 * bias.astype(jnp.float32)).astype(u.dtype)


def setup_inputs(seed: int = 0) -> dict:
    key = jax.random.key(seed)
    ks = iter(jax.random.split(key, 40))

    def nrm(shape, scale):
        return jax.random.normal(next(ks), shape, jnp.float32) * scale

    def gain(n):
        return 1.0 + nrm((DEPTH, n), 0.01)

    return {
        "x": nrm((BATCH, SEQ, D_MODEL), 1.0),
        "p": nrm((DEPTH, BATCH, SEQ, PLE_DIM), 1.0),
        "norm_mix": gain(D_MODEL),
        "w_in": nrm((DEPTH, D_MODEL, IN_PROJ_WIDTH), D_MODEL ** -0.5),
        "short_conv_w": nrm((DEPTH, 3, 3 * HYENA_WIDTH), 3 ** -0.5),
        "short_conv_b": nrm((DEPTH, 3 * HYENA_WIDTH), 0.01),
        "q_norm": gain(Q_LORA_RANK),
        "w_uq": nrm((DEPTH, Q_LORA_RANK, N_HEADS_MLA * QK_HEAD_DIM), Q_LORA_RANK ** -0.5),
        "kv_norm": gain(KV_LORA_RANK),
        "w_ukv": nrm((DEPTH, KV_LORA_RANK, N_HEADS_MLA * (QK_NOPE_DIM + V_HEAD_DIM)), KV_LORA_RANK ** -0.5),
        "qk_norm_q": gain(QK_HEAD_DIM),
        "qk_norm_k": gain(QK_HEAD_DIM),
        "filt_w1": nrm((DEPTH, FILTER_EMB_DIM, FILTER_ORDER), FILTER_EMB_DIM ** -0.5),
        "filt_b1": nrm((DEPTH, FILTER_ORDER), 0.01),
        "filt_w2": nrm((DEPTH, FILTER_ORDER, FILTER_ORDER), FILTER_ORDER ** -0.5),
        "filt_b2": nrm((DEPTH, FILTER_ORDER), 0.01),
        "filt_w3": nrm((DEPTH, FILTER_ORDER, FILTER_ORDER), FILTER_ORDER ** -0.5),
        "filt_b3": nrm((DEPTH, FILTER_ORDER), 0.01),
        "filt_w_out": nrm((DEPTH, FILTER_ORDER, 2 * HYENA_WIDTH), FILTER_ORDER ** -0.5),
        "filt_freq": gain(FILTER_ORDER),
        "hyena_bias": nrm((DEPTH, HYENA_WIDTH), 1.0),
        "out_norm_attn": gain(MLA_WIDTH),
        "out_norm_hyena": gain(HYENA_WIDTH),
        "w_out": nrm((DEPTH, MIX_WIDTH, D_MODEL), MIX_WIDTH ** -0.5),
        "norm_ffn": gain(D_MODEL),
        "w_up": nrm((DEPTH, D_MODEL, 2 * D_FF), D_MODEL ** -0.5),
        "ffn_conv_w": nrm((DEPTH, 3, 2 * D_FF), 3 ** -0.5),
        "ffn_conv_b": nrm((DEPTH, 2 * D_FF), 0.01),
        "w_down": nrm((DEPTH, D_FF, D_MODEL), D_FF ** -0.5),
        "w_ple": nrm((DEPTH, PLE_DIM, D_MODEL), PLE_DIM ** -0.5),
        "w_ple_gate": nrm((DEPTH, D_MODEL, D_MODEL), D_MODEL ** -0.5),
        "ple_norm": gain(D_MODEL),
    }


def reference(x, p, norm_mix, w_in, short_conv_w, short_conv_b, q_norm, w_uq, kv_norm, w_ukv,
              qk_norm_q, qk_norm_k, filt_w1, filt_b1, filt_w2, filt_b2, filt_w3, filt_b3,
              filt_w_out, filt_freq, hyena_bias, out_norm_attn, out_norm_hyena, w_out,
              norm_ffn, w_up, ffn_conv_w, ffn_conv_b, w_down, w_ple, w_ple_gate, ple_norm):
    B, S, _ = x.shape
    H = N_HEADS_MLA
    pos = jnp.arange(S, dtype=jnp.float32)
    inv_freq = ROPE_BASE ** (-jnp.arange(0, QK_ROPE_DIM, 2, dtype=jnp.float32) / QK_ROPE_DIM)
    ang = pos[:, None] * inv_freq[None, :]
    cos = jnp.cos(ang)[:, None, :]
    sin = jnp.sin(ang)[:, None, :]
    c1 = Q_LORA_RANK
    c2 = c1 + KV_LORA_RANK
    c3 = c2 + QK_ROPE_DIM

    for i in range(DEPTH):
        h = _rmsnorm(x, norm_mix[i])
        proj = h @ w_in[i]
        c_q, c_kv, k_pe, u_h = proj[..., :c1], proj[..., c1:c2], proj[..., c2:c3], proj[..., c3:]

        q = (_rmsnorm(c_q, q_norm[i]) @ w_uq[i]).reshape(B, S, H, QK_HEAD_DIM)
        kv = (_rmsnorm(c_kv, kv_norm[i]) @ w_ukv[i]).reshape(B, S, H, QK_NOPE_DIM + V_HEAD_DIM)
        k_nope, v = kv[..., :QK_NOPE_DIM], kv[..., QK_NOPE_DIM:]
        k_pe_h = jnp.broadcast_to(k_pe[:, :, None, :], (B, S, H, QK_ROPE_DIM))
        k = jnp.concatenate([k_nope, k_pe_h], axis=-1)
        q = _rmsnorm(q, qk_norm_q[i])
        k = _rmsnorm(k, qk_norm_k[i])
        q = jnp.concatenate([q[..., :QK_NOPE_DIM], _rope(q[..., QK_NOPE_DIM:], cos, sin)], axis=-1)
        k = jnp.concatenate([k[..., :QK_NOPE_DIM], _rope(k[..., QK_NOPE_DIM:], cos, sin)], axis=-1)
        y_attn = _block_attention(q, k, v).reshape(B, S, MLA_WIDTH)

        u_h = _dwconv3(u_h, short_conv_w[i], short_conv_b[i])
        x0, x1, vh = jnp.split(u_h, 3, axis=-1)
        h_fwd, h_bwd = _hyena_filters(S, filt_w1[i], filt_b1[i], filt_w2[i], filt_b2[i],
                                      filt_w3[i], filt_b3[i], filt_w_out[i], filt_freq[i])
        y_hyena = x0 * _bidir_long_conv(x1 * vh, h_fwd, h_bwd, hyena_bias[i])

        y_mix = jnp.concatenate([_rmsnorm(y_attn, out_norm_attn[i]),
                                 _rmsnorm(y_hyena, out_norm_hyena[i])], axis=-1)
        x = x + y_mix @ w_out[i]

        h = _rmsnorm(x, norm_ffn[i])
        up = _dwconv3(h @ w_up[i], ffn_conv_w[i], ffn_conv_b[i])
        g, u = up[..., :D_FF], up[..., D_FF:]
        x = x + (jax.nn.silu(g) * u) @ w_down[i]

        e = _rmsnorm(p[i] @ w_ple[i], ple_norm[i])
        x = x + jax.nn.sigmoid(x @ w_ple_gate[i]) * e
    return x
```

```python
import numpy as np
import ml_dtypes
from contextlib import ExitStack
import concourse.bass as bass
import concourse.mybir as mybir
from concourse.bass_utils import run_bass_kernel_spmd

F32 = mybir.dt.float32
BF16 = mybir.dt.bfloat16
ALU = mybir.AluOpType
AF = mybir.ActivationFunctionType

ENGS = ['pe', 'act', 'dve', 'pool', 'sp']
N_DMA_SEMS = 24


class Sched:
    def __init__(self, nc):
        self.nc = nc
        self.ops = {e: [] for e in ENGS}
        self.last_w = {}
        self.readers = {}
        self.n_dma = 0
        self.dma_tok_of_slot = {}
        self.frontier = set()
        self.known = set()
        self.stopped = False

    def phase(self):
        fr = set()
        for e in ENGS:
            if self.ops[e]:
                for i in range(len(self.ops[e]) - 1, -1, -1):
                    if self.ops[e][i]['dma'] is None and self.ops[e][i]['fn'] is not None:
                        fr.add((e, i))
                        break
        for n in range(max(0, self.n_dma - N_DMA_SEMS), self.n_dma):
            fr.add(('dma', n))
        self.frontier = fr
        self.known = set()

    def _deps(self, eng, reads, writes):
        deps = set()
        for k in reads:
            t = self.last_w.get(k)
            if t is not None:
                deps.add(t)
        for k in writes:
            t = self.last_w.get(k)
            if t is not None:
                deps.add(t)
            for r in self.readers.get(k, {}).values():
                deps.add(r)
            if k not in self.known:
                deps |= self.frontier
        return deps

    def _commit(self, tok, reads, writes):
        for k in writes:
            self.known.add(k)
            self.last_w[k] = tok
            self.readers[k] = {}
        for k in reads:
            self.readers.setdefault(k, {})[tok if tok[0] == 'dma' else tok[0]] = tok

    def retire(self, keys):
        toks = set()
        for k in keys:
            t = self.last_w.pop(k, None)
            if t is not None:
                toks.add(t)
            for r in self.readers.pop(k, {}).values():
                toks.add(r)
        return toks

    def adopt(self, keys, toks):
        for k in keys:
            d = self.readers.setdefault(k, {})
            for t in toks:
                d[('x', t)] = t

    def op(self, eng, fn, reads=(), writes=()):
        if self.stopped:
            return None
        deps = self._deps(eng, reads, writes)
        idx = len(self.ops[eng])
        tok = (eng, idx)
        if eng == 'pe':
            deps = {d for d in deps if d[0] != 'pe'}
        deps.discard(tok)
        self.ops[eng].append(dict(fn=fn, deps=deps, dma=None))
        self._commit(tok, reads, writes)
        return tok

    def dma(self, fn, reads=(), writes=(), q='sp'):
        if self.stopped:
            return None
        deps = self._deps(q, reads, writes)
        n = self.n_dma
        self.n_dma += 1
        slot = n % N_DMA_SEMS
        prev = self.dma_tok_of_slot.get(slot)
        if prev is not None:
            deps.add(prev)
        tok = ('dma', n)
        self.dma_tok_of_slot[slot] = tok
        self.ops[q].append(dict(fn=fn, deps=deps, dma=n))
        self._commit(tok, reads, writes)
        return tok

    def finish(self, toks, q='sp'):
        if self.stopped:
            return
        self.ops[q].append(dict(fn=None, deps=set(toks), dma=None))

    def emit(self, block, sems, dma_sems):
        if self.stopped:
            return
        nc = self.nc
        mile = {e: {} for e in ENGS}
        needed = {e: set() for e in ENGS}
        for e in ENGS:
            for o in self.ops[e]:
                for d in o['deps']:
                    if d[0] != 'dma':
                        needed[d[0]].add(d[1])
        for e in ENGS:
            c = 0
            for i in range(len(self.ops[e])):
                if i in needed[e]:
                    c += 1
                    mile[e][i] = c
        self.mile = mile

        def run(ename, engine):
            waited = {}
            for i, o in enumerate(self.ops[ename]):
                want = {}
                for d in o['deps']:
                    if d[0] == 'dma':
                        n = d[1]
                        s = ('d', n % N_DMA_SEMS)
                        v = 16 * (n // N_DMA_SEMS + 1)
                    else:
                        s = ('e', d[0])
                        v = mile[d[0]][d[1]]
                    if v > want.get(s, 0):
                        want[s] = v
                for s, v in want.items():
                    if waited.get(s, 0) >= v:
                        continue
                    waited[s] = v
                    sem = dma_sems[s[1]] if s[0] == 'd' else sems[s[1]]
                    engine.wait_ge(sem, v)
                if o['fn'] is None:
                    continue
                ins = o['fn'](engine)
                if o['dma'] is not None:
                    ins.then_inc(dma_sems[o['dma'] % N_DMA_SEMS], 16)
                elif i in mile[ename]:
                    ins.then_inc(sems[ename], 1)

        @block.tensor
        def _(e):
            run('pe', e)

        @block.scalar
        def _(e):
            run('act', e)

        @block.vector
        def _(e):
            run('dve', e)

        @block.gpsimd
        def _(e):
            run('pool', e)

        @block.sync
        def _(e):
            run('sp', e)


T = 2048
NTC = 4
TCW = 512
D = 1024
EPS = 1e-6
DFF = 2816
NFF = 22

ARENA_W = 52480
NDUMMY = 0
B1_BARRIERS = True
B1_MASK = (0, 0, 0, 0, 0)
B1_TARGETED = True

VEC_LAYOUT = [
    ('norm_mix', 8), ('q_norm', 6), ('kv_norm', 2), ('gq', 1), ('gk', 1),
    ('scw0', 12), ('scw1', 12), ('scw2', 12), ('scb', 12), ('hbias', 4),
    ('g_attn', 4), ('g_hy', 4), ('norm_ffn', 8),
    ('fcw0', 44), ('fcw1', 44), ('fcw2', 44), ('fcb', 44), ('ple_norm', 8),
    ('fb1', 1), ('fb2', 1), ('fb3', 1), ('ffr', 1), ('eps', 1), ('wf0', 1), ('wf1', 1),
]
VCOL = {}
_c = 0
for _n, _k in VEC_LAYOUT:
    VCOL[_n] = _c
    _c += _k
NV = _c
NVX = NV + 16


class _Done(Exception):
    pass


def build_program(dbg=None):
    nc = bass.Bass("TRN2", target_bir_lowering=False)

    def din(name, shape, dt=F32):
        return nc.dram_tensor(name, list(shape), dt, kind="ExternalInput").ap()

    xT_d = din("xT", [D, T])
    pT_d = din("pT", [256, T])
    w_in_d = din("w_in", [D, 2592])
    w_uq_d = din("w_uq", [768, 768])
    w_ukv_d = din("w_ukv", [256, 1024])
    w_out_d = din("w_out", [D, D])
    w_up_d = din("w_up", [D, 2 * DFF])
    w_down_d = din("w_down", [DFF, D])
    w_ple_d = din("w_ple", [256, D])
    w_gate_d = din("w_gate", [D, D])
    fw1_d = din("fw1", [33, 64])
    fw2_d = din("fw2", [64, 64])
    fw3_d = din("fw3", [64, 64])
    fwo_d = din("fwo", [64, 1024])
    vecs_d = din("vecs", [128, NV])
    cbf_d = din("cbf", [128, 736], BF16)
    altrow_d = din("altrow", [1, T], BF16)
    rope_d = din("rope", [2, 96, T])
    zT_d = din("zT", [33, T])
    decay_d = din("decay", [T, 512])
    ctf_d = din("ctf", [16, 128, 2048], BF16)
    stf_d = din("stf", [16, 128, 2048], BF16)
    cinv_d = din("cinv", [T, T], BF16)
    sinv_d = din("sinv", [T, T], BF16)
    outT_d = nc.dram_tensor("outT", [D, T], F32, kind="ExternalOutput").ap()
    dbg_d = None
    if dbg is not None:
        dbg_d = nc.dram_tensor("dbg", [128, dbg[1]], F32, kind="ExternalOutput").ap()

    S = Sched(nc)
    with ExitStack() as es:
        arena = es.enter_context(nc.sbuf_tensor("arena", [128, ARENA_W], F32))
        ps = es.enter_context(nc.psum_tensor("ps", [128, 4096], F32))
        sems = {e: es.enter_context(nc.semaphore("s_" + e)) for e in ENGS}
        dsems = [es.enter_context(nc.semaphore("d%d" % i)) for i in range(N_DMA_SEMS)]
        block = es.enter_context(nc.Block())

        def V(off, dt, *shape):
            n = 1
            for s_ in shape:
                n *= s_
            words = n if dt == F32 else (n + 1) // 2
            assert off + words <= ARENA_W, (off, words)
            ap = arena[:, off:off + words]
            if dt == BF16:
                ap = ap.bitcast(BF16)
            if len(shape) == 2:
                ap = ap.rearrange("p (a b) -> p a b", b=shape[1])
            elif len(shape) == 3:
                ap = ap.rearrange("p (a b c) -> p a b c", b=shape[1], c=shape[2])
            return ap

        def bank(b, p=128, n=512, o=0):
            return ps[0:p, b * 512 + o: b * 512 + o + n]

        bctr = [0]

        def nb():
            b = bctr[0] % 8
            bctr[0] += 1
            return b

        def PK(b):
            return ('ps', b)

        cbf = V(0, BF16, 736)
        ident = cbf[:, 0:128]
        ones = cbf[:, 128:256]
        Rrot = cbf[0:96, 256:352]
        Esh = cbf[:, 352:480]
        altc = cbf[:, 448:449]
        ones96 = cbf[:, 480:608]
        Rrot128 = cbf[:, 608:736]
        vec = V(368, F32, NVX)
        o_ = 368 + NVX
        altrow = V(o_, BF16, T)
        o_ += 1024
        ynyq = V(o_, BF16, 512)
        o_ += 256
        assert o_ <= 2048, o_
        BASE = 2048

        def vc(name, i=0, p=128):
            c = VCOL[name] + i
            return vec[0:p, c:c + 1]

        def vx(i, p=128):
            return vec[0:p, NV + i:NV + i + 1]

        for _i in range(NDUMMY):
            S.op('dve', lambda e: e.memset(arena[:, 2040:2044], 0.0), writes=['DUMMY'])
        S.dma(lambda e: e.dma_start(out=cbf, in_=cbf_d), writes=['cbf'])
        S.dma(lambda e: e.dma_start(out=vec[:, 0:NV], in_=vecs_d), writes=['vec'])
        S.dma(lambda e: e.dma_start(out=altrow[0:1, :], in_=altrow_d), writes=['altrow'])

        def mm(out, lhsT, rhs, start, stop, reads, writes):
            S.op('pe', lambda e: e.matmul(out, lhsT=lhsT, rhs=rhs, start=start, stop=stop), reads=reads, writes=writes)

        def act(out, in_, func, reads, writes, scale=1.0, bias=None, accum_out=None):
            kw = {}
            if bias is not None:
                kw['bias'] = bias
            if accum_out is not None:
                kw['accum_out'] = accum_out
            S.op('act', lambda e: e.activation(out=out, in_=in_, func=func, scale=scale, **kw), reads=reads, writes=writes)

        def stt(eng, out, in0, scalar, in1, op0, op1, reads, writes):
            S.op(eng, lambda e: e.scalar_tensor_tensor(out=out, in0=in0, scalar=scalar, in1=in1, op0=op0, op1=op1), reads=reads, writes=writes)

        def tt(eng, out, in0, in1, op, reads, writes):
            S.op(eng, lambda e: e.tensor_tensor(out=out, in0=in0, in1=in1, op=op), reads=reads, writes=writes)

        def ts(eng, out, in0, s1, s2, op0, op1, reads, writes):
            if s2 is None:
                S.op(eng, lambda e: e.tensor_scalar(out=out, in0=in0, scalar1=s1, scalar2=None, op0=op0), reads=reads, writes=writes)
            else:
                S.op(eng, lambda e: e.tensor_scalar(out=out, in0=in0, scalar1=s1, scalar2=s2, op0=op0, op1=op1), reads=reads, writes=writes)

        def cp(eng, out, in_, reads, writes):
            if eng == 'act':
                S.op('act', lambda e: e.copy(out=out, in_=in_), reads=reads, writes=writes)
            else:
                S.op(eng, lambda e: e.tensor_copy(out=out, in_=in_), reads=reads, writes=writes)

        def rstd_from_bank(b, Dn, dst, p, dkey, extra=()):
            act(dst, bank(b, p), AF.Ln, [PK(b), 'vec'] + list(extra), [dkey], scale=1.0 / Dn, bias=vc('eps', 0, p))
            act(dst, dst, AF.Exp, [dkey], [dkey], scale=-0.5)

        dbg_done = [False]

        def dump(tag, ap, ncols, keys, p=128):
            if dbg is None or dbg[0] != tag or dbg_done[0]:
                return
            dbg_done[0] = True
            tmp = V(ARENA_W - dbg[1], F32, dbg[1])
            cp('dve', tmp[0:p, 0:ncols], ap, keys, ['dbgtmp'])
            S.dma(lambda e: e.dma_start(out=dbg_d[0:p, 0:ncols], in_=tmp[0:p, 0:ncols]), reads=['dbgtmp'], writes=['dbgout'])
            finish_program()
            S.stopped = True

        def finish_program(extra=()):
            toks = [('dma', n) for n in range(max(0, S.n_dma - N_DMA_SEMS), S.n_dma)]
            S.finish(toks)
            S.emit(block, sems, dsems)

        uT = V(BASE, BF16, 4, T)
        x0T = V(BASE + 4096, BF16, 4, T)
        R1 = BASE + 8192
        cqT = V(R1, BF16, 6, T)
        ckvT = V(R1 + 6144, BF16, 2, T)
        kpeT = V(R1 + 8192, BF16, T)
        ymixT = V(R1, BF16, 8, T)
        R2 = R1 + 9216

        hT = V(R2, BF16, 8, T)
        XS = R2 + 8192
        xst = [V(XS, F32, 8, TCW), V(XS + 4096, F32, 8, TCW)]
        sqs = [V(XS + 8192, BF16, T), V(XS + 8192 + 1024, BF16, T)]
        rss = [V(XS + 10240, F32, TCW), V(XS + 10240 + 512, F32, TCW)]
        ACC = XS + 11264
        accs = [V(ACC + i * 2048, F32, T) for i in range(6)]
        assert ACC + 12288 <= ARENA_W

        xT_v = xT_d.rearrange("(kc p) t -> p kc t", p=128)
        for tc in range(NTC):
            xs = xst[tc % 2]
            S.dma(lambda e, xs=xs, tc=tc: e.dma_start(out=xs, in_=xT_v[:, :, tc * TCW:(tc + 1) * TCW]), writes=[('xst', tc % 2)])
            b = nb()
            for kc in range(8):
                sq = sqs[kc % 2][:, 0:TCW]
                act(sq, xs[:, kc, :], AF.Square, [('xst', tc % 2)], [('sq', kc % 2)])
                mm(bank(b), ones, sq, kc == 0, kc == 7, [('sq', kc % 2), 'cbf'], [PK(b)])
            rs = rss[tc % 2]
            rstd_from_bank(b, D, rs, 128, ('rs', tc % 2))
            for kc in range(8):
                stt('dve', hT[:, kc, tc * TCW:(tc + 1) * TCW], xs[:, kc, :], vc('norm_mix', kc), rs, ALU.mult, ALU.mult,
                    [('xst', tc % 2), ('rs', tc % 2), 'vec'], [('hT', kc, tc)])
        dump('hT', hT[:, 0, 0:512], 512, [('hT', 0, 0)])

        WST = XS
        wst = [V(WST, F32, 8, 256), V(WST + 2048, F32, 8, 256)]
        wbf = [V(WST + 4096, BF16, 8, 256), V(WST + 5120, BF16, 8, 256)]
        S.phase()
        S.op('pool', lambda e: e.memset(kpeT, 0.0), writes=['kpe'])
        w_in_v = w_in_d.rearrange("(kc p) n -> p kc n", p=128)
        jobs = []
        for i in range(3):
            jobs.append(([(i * 256, 256)], [('cq', 2 * i), ('cq', 2 * i + 1)]))
        jobs.append(([(768, 256)], [('ckv', 0), ('ckv', 1)]))
        jobs.append(([(1024, 32)], [('kpe', 0)]))
        for i in range(2):
            jobs.append(([(1056 + i * 256, 256)], [('x0', 2 * i), ('x0', 2 * i + 1)]))
        for j in range(4):
            jobs.append(([(1056 + 512 + j * 128, 128), (1056 + 1024 + j * 128, 128)], [('x1', j), ('vh', j)]))
        half = [0]

        def conv3(psv, w0, w1, w2, bb, acc, akey, rkeys, p=128):
            act(acc, psv, AF.Identity, rkeys + ['vec'], [akey], scale=w1, bias=bb)
            stt('dve', acc[:, 1:T], psv[:, 0:T - 1], w0, acc[:, 1:T], ALU.mult, ALU.add, rkeys + [akey, 'vec'], [akey])
            stt('dve', acc[:, 0:T - 1], psv[:, 1:T], w2, acc[:, 0:T - 1], ALU.mult, ALU.add, rkeys + [akey, 'vec'], [akey])

        def pre_job(ji):
            segs, _ = jobs[ji]
            st = wst[ji % 2]
            wb = wbf[ji % 2]
            off = 0
            for (c0, n) in segs:
                S.dma(lambda e, st=st, off=off, c0=c0, n=n: e.dma_start(out=st[:, :, off:off + n], in_=w_in_v[:, :, c0:c0 + n]),
                      writes=[('wst', ji % 2)])
                off += n
            cp('act', wb[:, :, 0:off], st[:, :, 0:off], [('wst', ji % 2)], [('wbf', ji % 2)])

        pre_job(0)
        for ji, (segs, chunks) in enumerate(jobs):
            wb = wbf[ji % 2]
            if ji + 1 < len(jobs):
                pre_job(ji + 1)
            o = 0
            for (kind, idx) in chunks:
                M = 32 if kind == 'kpe' else 128
                h = half[0] % 2
                half[0] += 1
                for tc in range(NTC):
                    b = h * 4 + tc
                    for kc in range(8):
                        mm(bank(b, M), wb[:, kc, o:o + M], hT[:, kc, tc * TCW:(tc + 1) * TCW], kc == 0, kc == 7,
                           [('wbf', ji % 2), ('hT', kc, tc)], [PK(b)])
                o += M
                psv = ps[0:M, h * 2048:(h + 1) * 2048]
                pk = [PK(h * 4 + t_) for t_ in range(4)]
                if kind == 'cq':
                    cp('act', cqT[:, idx, :], psv, pk, [('cq', idx)])
                elif kind == 'ckv':
                    cp('act', ckvT[:, idx, :], psv, pk, [('ckv', idx)])
                elif kind == 'kpe':
                    cp('act', kpeT[0:32, :], psv, pk + ['kpe'], ['kpe'])
                else:
                    wi = {'x0': 0, 'x1': 4, 'vh': 8}[kind] + idx
                    ai = {'x0': 0, 'x1': 1, 'vh': 2}[kind]
                    sl_ = 3 * (idx % 2)
                    conv3(psv, vc('scw0', wi), vc('scw1', wi), vc('scw2', wi), vc('scb', wi), accs[sl_ + ai], ('acc', sl_ + ai), pk)
                    if kind == 'x0':
                        cp('act', x0T[:, idx, :], accs[sl_], [('acc', sl_)], [('x0T', idx)])
                    elif kind == 'vh':
                        tt('pool', uT[:, idx, :], accs[sl_ + 1], accs[sl_ + 2], ALU.mult, [('acc', sl_ + 1), ('acc', sl_ + 2)], [('uT', idx)])
        dump('cq', cqT[:, 0, 0:512], 512, [('cq', 0)])
        dump('kpe', kpeT[0:32, 0:512], 512, ['kpe'], p=32)
        dump('x0', x0T[:, 1, 0:512], 512, [('x0T', 1)])
        dump('u', uT[:, 3, 1536:2048], 512, [('uT', 3)])

        S.phase()
        qhat = V(R2, BF16, 8, T)
        khat = V(R2 + 8192, BF16, 8, T)
        VA = R2 + 16384
        Vaug = V(VA, BF16, 16, 8, 66)
        RP = VA + 4224
        ropeC = V(RP, F32, T)
        ropeS = V(RP + 2048, F32, T)
        WQ = RP + 4096
        wuq = V(WQ, BF16, 6, 800)
        wkn = V(WQ + 2400, BF16, 2, 800)
        wv = V(WQ + 3200, BF16, 2, 8, 64)
        TM = WQ + 3712
        wstg = V(TM, F32, 3, 768)
        assert TM + 4608 <= ARENA_W, TM
        ropeCS = V(RP, F32, 2 * T)
        S.op('pool', lambda e: e.memset(ropeCS, 0.0), writes=['ropeC', 'ropeS'])
        S.dma(lambda e: e.dma_start(out=ropeC[0:96, :], in_=rope_d[0]), writes=['ropeC'])
        S.dma(lambda e: e.dma_start(out=ropeS[0:96, :], in_=rope_d[1]), writes=['ropeS'])
        w_uq_v = w_uq_d.rearrange("(kc p) n -> p kc n", p=128)
        S.op('pool', lambda e: e.memset(wuq[:, :, 768:800], 0.0), writes=[('wuq', 0), ('wuq', 1)])
        for hh in range(2):
            S.dma(lambda e, hh=hh: e.dma_start(out=wstg, in_=w_uq_v[:, hh * 3:(hh + 1) * 3, :]), writes=['wstg'])
            cp('dve', wuq[:, hh * 3:(hh + 1) * 3, 0:768], wstg, ['wstg', ('wuq', hh)], [('wuq', hh)])
        wstg2 = V(TM, F32, 2, 8, 128)
        S.dma(lambda e: e.dma_start(out=wstg2, in_=w_ukv_d.rearrange("(kc p) (h c) -> p kc h c", p=128, c=128)), reads=[], writes=['wstg'])
        S.op('pool', lambda e: e.memset(wkn, 0.0), writes=['wkn'])
        for kc in range(2):
            cp('dve', wkn[:, kc, 0:768].rearrange("p (h c) -> p h c", c=96)[:, :, 0:64], wstg2[:, kc, :, 0:64], ['wstg', 'wkn'], ['wkn'])
            cp('dve', wv[:, kc, :, :], wstg2[:, kc, :, 64:128], ['wstg'], ['wv'])
        sqB = [V(TM, BF16, T), V(TM + 1024, BF16, T)]
        rsB = [V(TM + 2048, F32, TCW), V(TM + 2560, F32, TCW)]
        qbB = [V(TM + 3072, BF16, TCW), V(TM + 3328, BF16, TCW)]
        t1B = V(TM + 3584, F32, TCW)
        t2B = V(TM + 4096, F32, TCW)

        def fm_norm_inplace(buf, nch, Dn, gname, bkey, sqb, rsb, sqk, rsk, extra_reads=()):
            bs = [nb() for _ in range(4)]
            for kc in range(nch):
                sq = sqb[kc % 2]
                act(sq, buf[:, kc, :], AF.Square, [(bkey, kc)] + list(extra_reads), [(sqk, kc % 2)])
                for tc in range(NTC):
                    mm(bank(bs[tc]), ones, sq[:, tc * TCW:(tc + 1) * TCW], kc == 0, kc == nch - 1, [(sqk, kc % 2), 'cbf'], [PK(bs[tc])])
            for tc in range(NTC):
                rs = rsb[tc % 2]
                rstd_from_bank(bs[tc], Dn, rs, 128, (rsk, tc % 2))
                for kc in range(nch):
                    sl = buf[:, kc, tc * TCW:(tc + 1) * TCW]
                    stt('dve', sl, sl, vc(gname, kc), rs, ALU.mult, ALU.mult, [(bkey, kc), (rsk, tc % 2), 'vec'], [(bkey, kc)])

        S.phase()
        fm_norm_inplace(cqT, 6, 768, 'q_norm', 'cq', sqB, rsB, 'sqB', 'rsB')
        fm_norm_inplace(ckvT, 2, 256, 'kv_norm', 'ckv', sqB, rsB, 'sqB', 'rsB')
        dump('cqn', cqT[:, 5, 0:512], 512, [('cq', 5)])

        VaugF = V(VA, BF16, 16 * 8 * 66)
        wv2 = V(WQ + 3200, BF16, 2, 512)
        S.op('pool', lambda e: e.memset(VaugF, 1.0), writes=['Vones'] + [('V', t_) for t_ in range(16)])
        dump('Vm', VaugF[:, 0:512], 512, ['Vones'])
        for ttile in range(16):
            b = nb()
            for kc in range(2):
                mm(bank(b), ckvT[:, kc, ttile * 128:(ttile + 1) * 128], wv2[:, kc, :], kc == 0, kc == 1,
                   [('ckv', kc), 'wv'], [PK(b)])
            dump('Vp', bank(b), 512, [PK(b)])
            cp('dve', Vaug[:, ttile, :, 0:64], bank(b).rearrange("p (h c) -> p h c", c=64), [PK(b), ('V', ttile)], [('V', ttile)])

        free_banks = list(range(8))

        def fb():
            return free_banks.pop(0)

        def stage_a1(item):
            kind, h, tc = item
            b = fb()
            if kind == 'q':
                for kc in range(6):
                    mm(bank(b), wuq[:, kc, h * 96:h * 96 + 128], cqT[:, kc, tc * TCW:(tc + 1) * TCW], kc == 0, kc == 5,
                       [('wuq', kc // 3), ('cq', kc)], [PK(b)])
            else:
                for kc in range(2):
                    mm(bank(b), wkn[:, kc, h * 96:h * 96 + 128], ckvT[:, kc, tc * TCW:(tc + 1) * TCW], kc == 0, False, ['wkn', ('ckv', kc)], [PK(b)])
                mm(bank(b), Esh, kpeT[:, tc * TCW:(tc + 1) * TCW], False, True, ['cbf', 'kpe'], [PK(b)])
            return b

        def stage_a2(item, idx, b):
            kind, h, tc = item
            gname = 'gq' if kind == 'q' else 'gk'
            raw = bank(b)
            qb = qbB[idx % 2]
            act(qb, raw, AF.Copy, [PK(b), 'vec'], [('qb', idx % 2)], scale=vc(gname))
            sq = sqB[idx % 2][:, 0:TCW]
            act(sq, raw, AF.Square, [PK(b)], [('sqB', idx % 2)])
            b1 = fb()
            mm(bank(b1), ones96, sq, True, True, [('sqB', idx % 2), 'cbf'], [PK(b1)])
            b2 = fb()
            mm(bank(b2), Rrot128, qb, True, True, [('qb', idx % 2), 'cbf'], [PK(b2)])
            return (b, b1, b2, gname)

        def stage_b(item, idx, st):
            kind, h, tc = item
            b, b1, b2, gname = st
            raw = bank(b)
            dstbuf, dname = (qhat, 'qhat') if kind == 'q' else (khat, 'khat')
            dst = dstbuf[:, h, tc * TCW:(tc + 1) * TCW]
            rs = rsB[idx % 2]
            rstd_from_bank(b1, 96, rs, 128, ('rsB', idx % 2), extra=[PK(b2)])
            csl = ropeC[:, tc * TCW:(tc + 1) * TCW]
            ssl = ropeS[:, tc * TCW:(tc + 1) * TCW]
            t1, t2 = t1Bs[idx % 2], t2Bs[idx % 2]
            k1, k2 = ('t1B', idx % 2), ('t2B', idx % 2)
            stt('dve', t1, raw, vc(gname), csl, ALU.mult, ALU.mult, [PK(b), 'ropeC', 'vec', PK(b2)], [k1])
            tt('dve', t2, bank(b2), ssl, ALU.mult, [PK(b2), 'ropeS'], [k2])
            tt('dve', t1, t1, t2, ALU.add, [k1, k2], [k1])
            tt('pool', dst, t1, rs, ALU.mult, [k1, ('rsB', idx % 2)], [(dname, h, tc)])
            free_banks.extend([b, b1, b2])

        t1Bs = [t1B, V(TM + 256, F32, TCW)]
        t2Bs = [t2B, V(TM + 1024 + 256, F32, TCW)]
        S.phase()
        items = []
        for h in range(8):
            for tc in range(NTC):
                items.append(('q', h, tc))
                items.append(('k', h, tc))
        NI = len(items)
        raws = {0: stage_a1(items[0]), 1: stage_a1(items[1])}
        sts = {0: stage_a2(items[0], 0, raws[0])}
        for i in range(NI):
            if i + 2 < NI:
                raws[i + 2] = stage_a1(items[i + 2])
            if i + 1 < NI:
                sts[i + 1] = stage_a2(items[i + 1], i + 1, raws[i + 1])
            stage_b(items[i], i, sts[i])
            dump('c%d' % i, qhat[0:96, 0, 0:512], 512, [('qhat', 0, 0)], p=96)
        dump('khat', khat[0:96, 2, 512:1024], 512, [('khat', 2, 1)], p=96)
        dump('qhat', qhat[0:96, 7, 1536:2048], 512, [('qhat', 7, 3)], p=96)

        S.phase()
        PTB = RP
        NPT = 6
        pts = [V(PTB + i * 256, BF16, TCW) for i in range(NPT)]
        YAT = PTB + NPT * 256
        yat = V(YAT, F32, 16, 512)
        SM = YAT + 8192
        rcs = [V(SM + i * 4, F32, 4) for i in range(2)]
        ssqc = V(SM + 8, F32, 16)
        rsc = V(SM + 24, F32, 16)
        ynb = [V(SM + 40, BF16, 512), V(SM + 40 + 256, BF16, 512)]
        assert SM + 40 + 512 <= ARENA_W
        SCALE = 96.0 ** -0.5
        pt2 = [V(PTB + i * 512, BF16, 2 * TCW) for i in range(3)]
        pti = 0
        gi = 0
        sctr = [0]
        for h in range(8):
            for qc in range(NTC):
                gi += 1
                slot_of = {}

                def s_pair(kp, h=h, qc=qc):
                    slot = sctr[0] % 2
                    sctr[0] += 1
                    slot_of[kp] = slot
                    for u in range(2):
                        kt = 2 * kp + u
                        b = 2 * slot + u
                        mm(bank(b), khat[0:96, h, kt * 128:(kt + 1) * 128], qhat[0:96, h, qc * TCW:(qc + 1) * TCW], True, True,
                           [('khat', h, kt // 4), ('qhat', h, qc)], [PK(b)])
                s_pair(0)
                for kp in range(8):
                    if kp + 1 < 8:
                        s_pair(kp + 1)
                    slot = slot_of[kp]
                    pt = pt2[pti % 3]
                    pk = ('pt', pti % 3)
                    pti += 1
                    act(pt, ps[:, (2 * slot) * 512:(2 * slot + 2) * 512], AF.Exp, [PK(2 * slot), PK(2 * slot + 1)], [pk], scale=SCALE)
                    for u in range(2):
                        kt = 2 * kp + u
                        for j in range(4):
                            mm(bank(4 + j, 128, 65), pt[:, u * TCW + j * 128:u * TCW + (j + 1) * 128], Vaug[:, kt, h, 0:65], kt == 0, kt == 15,
                               [pk, ('V', kt), 'Vones'], [PK(4 + j)])
                rc = rcs[gi % 2]
                for j in range(4):
                    S.op('dve', lambda e, rc=rc, j=j: e.reciprocal(out=rc[:, j:j + 1], in_=bank(4 + j, 128, 1, 64)), reads=[PK(4 + j)], writes=[('rc', gi % 2, j)])
                    ts('dve', yat[:, qc * 4 + j, h * 64:(h + 1) * 64], bank(4 + j, 128, 64), rc[:, j:j + 1], None, ALU.mult, None,
                       [PK(4 + j), ('rc', gi % 2, j)], [('yat', qc * 4 + j)])
        dump('yat', yat[:, 5, :], 512, [('yat', 5)])

        S.phase()
        for qt in range(16):
            act(ynb[qt % 2], yat[:, qt, :], AF.Square, [('yat', qt)], [('ynj', qt % 2), ('ssq', qt)], accum_out=ssqc[:, qt:qt + 1])
        act(rsc[:, 0:16], ssqc[:, 0:16], AF.Ln, [('ssq', q_) for q_ in range(16)] + ['vec'], ['rsc'], scale=1.0 / 512, bias=vc('eps'))
        act(rsc[:, 0:16], rsc[:, 0:16], AF.Exp, ['rsc'], ['rsc'], scale=-0.5)
        ynb3 = [V(SM + 552 + i * 256, BF16, 512) for i in range(3)]
        for qt in range(16):
            yn = ynb3[qt % 3]
            ts('dve', yn, yat[:, qt, :], rsc[:, qt:qt + 1], None, ALU.mult, None, [('yat', qt), 'rsc'], [('yn3', qt % 3)])
            b = nb()
            for c in range(4):
                mm(bank(b, 128, 128, c * 128), yn[:, c * 128:(c + 1) * 128], ident, True, True, [('yn3', qt % 3), 'cbf'], [PK(b)])
            for c in range(4):
                act(ymixT[:, c, qt * 128:(qt + 1) * 128], bank(b, 128, 128, c * 128), AF.Copy, [PK(b), 'vec'], [('ymix', c)],
                    scale=vc('g_attn', c))
        dump('ymixa', ymixT[:, 2, 0:512], 512, [('ymix', 2)])

        S.phase()
        utok = V(R2, BF16, 16, 512)
        hsum = V(R2 + 4096, BF16, 16, 512)
        hdiff = V(R2 + 8192, BF16, 16, 512)
        FT = R2 + 12288
        zT = V(FT, F32, T)
        hA = V(FT + 2048, F32, T)
        YB = R2 + 16384
        Ybuf = V(YB, BF16, 32, 512)
        TB = YB + 8192
        hBf = V(TB, F32, T)
        s4b = V(TB + 2048, F32, T)
        fws = V(TB + 4096, F32, 64 * 3 + 1024)
        dcs = [V(TB + 4096 + 1216, F32, 512), V(TB + 4096 + 1216 + 512, F32, 512)]
        hfb = [V(TB + 4096 + 2240 + i * 512, F32, 512) for i in range(4)]
        assert TB + 4096 + 2240 + 2048 <= ARENA_W, TB
        S.dma(lambda e: e.dma_start(out=zT[0:33, :], in_=zT_d), writes=['zT'])
        S.dma(lambda e: e.dma_start(out=fws[0:33, 0:64], in_=fw1_d), writes=['fw1'])
        S.dma(lambda e: e.dma_start(out=fws[0:64, 64:128], in_=fw2_d), writes=['fw2'])
        S.dma(lambda e: e.dma_start(out=fws[0:64, 128:192], in_=fw3_d), writes=['fw3'])
        S.dma(lambda e: e.dma_start(out=fws[0:64, 192:1216], in_=fwo_d), writes=['fwo'])
        ts('dve', vx(0, 64), vc('ffr', 0, 64), 0.5, None, ALU.mult, None, ['vec'], ['vx'])
        ts('dve', vx(1, 64), vc('ffr', 0, 64), 0.25, None, ALU.mult, None, ['vec', 'vx'], ['vx'])
        for l in range(3):
            tt('dve', vx(2 + l, 64), vx(0, 64), vc('fb%d' % (l + 1), 0, 64), ALU.mult, ['vec', 'vx'], ['vx'])
            tt('dve', vx(5 + l, 64), vx(1, 64), vc('fb%d' % (l + 1), 0, 64), ALU.mult, ['vec', 'vx'], ['vx'])
        prev, pkey, Kp = zT, 'zT', 33
        outs = [hA, hBf, hA]
        for l in range(3):
            wl = fws[0:Kp, l * 64:(l + 1) * 64]
            ho = outs[l]
            hk = 'hA' if ho is hA else 'hB'
            for tc in range(NTC):
                b = nb()
                rk = ['zT'] if l == 0 else [(pkey, tc)]
                mm(bank(b, 64), wl, prev[0:Kp, tc * TCW:(tc + 1) * TCW], True, True, ['fw%d' % (l + 1)] + rk, [PK(b)])
                sl = slice(tc * TCW, (tc + 1) * TCW)
                act(ho[0:64, sl], bank(b, 64), AF.Sin, [PK(b), 'vx'], [(hk, tc)], scale=vx(0, 64), bias=vx(2 + l, 64))
                act(s4b[0:64, sl], bank(b, 64), AF.Sin, [PK(b), 'vx'], [('s4', tc)], scale=vx(1, 64), bias=vx(5 + l, 64))
                tt('dve', s4b[0:64, sl], s4b[0:64, sl], s4b[0:64, sl], ALU.mult, [('s4', tc)], [('s4', tc)])
                ts('dve', s4b[0:64, sl], s4b[0:64, sl], -2.0, 1.0, ALU.mult, ALU.add, [('s4', tc)], [('s4', tc)])
                stt('dve', ho[0:64, sl], ho[0:64, sl], 2.0, s4b[0:64, sl], ALU.mult, ALU.mult, [(hk, tc), ('s4', tc)], [(hk, tc)])
            prev, pkey, Kp = ho, hk, 64
        dump('G', hA[0:64, 0:512], 512, [('hA', 0)], p=64)
        for jt in range(16):
            dc = dcs[jt % 2]
            S.dma(lambda e, dc=dc, jt=jt: e.dma_start(out=dc, in_=decay_d[jt * 128:(jt + 1) * 128, :]), writes=[('dc', jt % 2)])
            bf_, bb_ = nb(), nb()
            gk = [('hA', jt // 4)]
            mm(bank(bf_), hA[0:64, jt * 128:(jt + 1) * 128], fws[0:64, 192:192 + 512], True, True, gk + ['fwo'], [PK(bf_)])
            mm(bank(bb_), hA[0:64, jt * 128:(jt + 1) * 128], fws[0:64, 192 + 512:192 + 1024], True, True, gk + ['fwo'], [PK(bb_)])
            hs = 2 * (jt % 2)
            hf_, hb_ = hfb[hs], hfb[hs + 1]
            kf_, kb_ = ('hf', hs), ('hf', hs + 1)
            tt('dve', hf_, bank(bf_), dc, ALU.mult, [PK(bf_), ('dc', jt % 2)], [kf_])
            tt('dve', hb_, bank(bb_), dc, ALU.mult, [PK(bb_), ('dc', jt % 2)], [kb_])
            if jt == 0:
                S.op('dve', lambda e, hb_=hb_: e.memset(hb_[0:1, :], 0.0), reads=[kb_], writes=[kb_])
            tt('pool', hsum[:, jt, :], hf_, hb_, ALU.add, [kf_, kb_], [('hsum', jt)])
            tt('pool', hdiff[:, jt, :], hf_, hb_, ALU.subtract, [kf_, kb_], [('hdiff', jt)])
        dump('hsum', hsum[:, 3, :], 512, [('hsum', 3)])

        S.phase()
        ctab = [V(TB, BF16, 16, 128), V(TB + 1024, BF16, 16, 128)]
        stab = [V(TB + 2048, BF16, 16, 128), V(TB + 3072, BF16, 16, 128)]
        KS = TB + 4096
        kcs = V(KS, F32, 512)
        kss = V(KS + 512, F32, 512)
        tq = [V(KS + 1024 + i * 512, F32, 512) for i in range(4)]
        nyk = V(KS + 3072, F32, 512)
        assert KS + 3584 <= ARENA_W
        for tk in range(16):
            b = nb()
            for c in range(4):
                mm(bank(b, 128, 128, c * 128), uT[:, c, tk * 128:(tk + 1) * 128], ident, True, True, [('uT', c), 'cbf'], [PK(b)])
            cp('act', utok[:, tk, :], bank(b), [PK(b)], [('utok', tk)])
        dump('utok', utok[:, 7, :], 512, [('utok', 7)])
        bu, bk = nb(), nb()
        for tk in range(16):
            mm(bank(bu, 1), altc, utok[:, tk, :], tk == 0, tk == 15, ['cbf', ('utok', tk)], [PK(bu)])
        for tk in range(16):
            mm(bank(bk, 1), altc, hsum[:, tk, :], tk == 0, tk == 15, ['cbf', ('hsum', tk)], [PK(bk)])
        cp('act', nyk[0:1, :], bank(bk, 1), [PK(bk)], ['nyk'])
        stt('dve', ynyq[0:1, :], bank(bu, 1), 1.0 / 4096, nyk[0:1, :], ALU.mult, ALU.mult, [PK(bu), 'nyk'], ['ynyq'])
        for m in range(16):
            ct, st_ = ctab[m % 2], stab[m % 2]
            S.dma(lambda e, ct=ct, m=m: e.dma_start(out=ct, in_=ctf_d[m].rearrange("p (k f) -> p k f", f=128)), writes=[('ctab', m % 2)])
            S.dma(lambda e, st_=st_, m=m: e.dma_start(out=st_, in_=stf_d[m].rearrange("p (k f) -> p k f", f=128)), writes=[('stab', m % 2)])
            hb_ = (m % 2) * 4
            b_uc, b_kc, b_us, b_ks = hb_, hb_ + 1, hb_ + 2, hb_ + 3
            for k in range(16):
                mm(bank(b_uc), ct[:, k, :], utok[:, k, :], k == 0, k == 15, [('ctab', m % 2), ('utok', k)], [PK(b_uc)])
                mm(bank(b_kc), ct[:, k, :], hsum[:, k, :], k == 0, k == 15, [('ctab', m % 2), ('hsum', k)], [PK(b_kc)])
            for k in range(16):
                mm(bank(b_us), st_[:, k, :], utok[:, k, :], k == 0, k == 15, [('stab', m % 2), ('utok', k)], [PK(b_us)])
                mm(bank(b_ks), st_[:, k, :], hdiff[:, k, :], k == 0, k == 15, [('stab', m % 2), ('hdiff', k)], [PK(b_ks)])
            wfc = vc('wf0') if m == 0 else vc('wf1')
            act(kcs, bank(b_kc), AF.Copy, [PK(b_kc), 'vec'], ['kcs'], scale=wfc)
            act(kss, bank(b_ks), AF.Copy, [PK(b_ks), 'vec'], ['kss'], scale=wfc)
            tt('dve', tq[0], bank(b_uc), kcs, ALU.mult, [PK(b_uc), 'kcs'], ['tq0'])
            tt('dve', tq[1], bank(b_us), kss, ALU.mult, [PK(b_us), 'kss'], ['tq1'])
            tt('pool', Ybuf[:, m, :], tq[0], tq[1], ALU.subtract, ['tq0', 'tq1'], [('Y', m)])
            tt('dve', tq[2], bank(b_uc), kss, ALU.mult, [PK(b_uc), 'kss'], ['tq2'])
            tt('dve', tq[3], bank(b_us), kcs, ALU.mult, [PK(b_us), 'kcs'], ['tq3'])
            tt('pool', Ybuf[:, 16 + m, :], tq[2], tq[3], ALU.add, ['tq2', 'tq3'], [('Y', 16 + m)])
        dump('Y', Ybuf[:, 1, :], 512, [('Y', 1)])

        S.phase()
        itab = [V(TB + i * 1024, BF16, T) for i in range(4)]
        ytmp = V(TB + 4096, F32, T)
        assert TB + 6144 <= ARENA_W
        xT = V(R2, F32, 8, T)
        for sw in range(2):
            for fi in range(32):
                it = itab[fi % 4]
                src = cinv_d if fi < 16 else sinv_d
                r0 = (fi % 16) * 128
                S.dma(lambda e, it=it, src=src, r0=r0: e.dma_start(out=it, in_=src[r0:r0 + 128, :]), writes=[('itab', fi % 4)])
                for ci in range(2):
                    c = sw * 2 + ci
                    for tq_ in range(4):
                        b = ci * 4 + tq_
                        mm(bank(b), Ybuf[:, fi, c * 128:(c + 1) * 128], it[:, tq_ * TCW:(tq_ + 1) * TCW], fi == 0, False,
                           [('Y', fi), ('itab', fi % 4)], [PK(b)])
            for ci in range(2):
                c = sw * 2 + ci
                for tq_ in range(4):
                    b = ci * 4 + tq_
                    mm(bank(b), ynyq[0:1, c * 128:(c + 1) * 128], altrow[0:1, tq_ * TCW:(tq_ + 1) * TCW], False, True,
                       ['ynyq', 'altrow'], [PK(b)])
                psv = ps[:, ci * 2048:(ci + 1) * 2048]
                pk = [PK(ci * 4 + t_) for t_ in range(4)]
                stt('dve', ytmp, uT[:, c, :], vc('hbias', c), psv, ALU.mult, ALU.add, pk + [('uT', c), 'vec'], ['ytmp'])
                tt('pool', ymixT[:, 4 + c, :], ytmp, x0T[:, c, :], ALU.mult, ['ytmp', ('x0T', c)], [('ymix', 4 + c)])
        dump('yhy', ymixT[:, 5, 0:512], 512, [('ymix', 5)])
        for g in range(4):
            S.dma(lambda e, g=g: e.dma_start(out=xT[:, 2 * g:2 * g + 2, :], in_=xT_v[:, 2 * g:2 * g + 2, :]), writes=[('xT', 2 * g, t_) for t_ in range(4)] + [('xT', 2 * g + 1, t_) for t_ in range(4)])
        S.phase()
        sqD = [V(TB, BF16, T), V(TB + 1024, BF16, T)]
        rsD = [V(TB + 2048, F32, TCW), V(TB + 2560, F32, TCW)]

        class _Shift:
            def __init__(self, buf, o):
                self.buf, self.o = buf, o

            def __getitem__(self, idx):
                return self.buf[idx[0], idx[1] + self.o, idx[2]]

        def fm_norm_hy():
            bs = [nb() for _ in range(4)]
            for kc in range(4):
                sq = sqD[kc % 2]
                act(sq, ymixT[:, 4 + kc, :], AF.Square, [('ymix', 4 + kc)], [('sqD', kc % 2)])
                for tc in range(NTC):
                    mm(bank(bs[tc]), ones, sq[:, tc * TCW:(tc + 1) * TCW], kc == 0, kc == 3, [('sqD', kc % 2), 'cbf'], [PK(bs[tc])])
            for tc in range(NTC):
                rs = rsD[tc % 2]
                rstd_from_bank(bs[tc], 512, rs, 128, ('rsD', tc % 2))
                for kc in range(4):
                    sl = ymixT[:, 4 + kc, tc * TCW:(tc + 1) * TCW]
                    stt('dve', sl, sl, vc('g_hy', kc), rs, ALU.mult, ALU.mult, [('ymix', 4 + kc), ('rsD', tc % 2), 'vec'], [('ymix', 4 + kc)])
        fm_norm_hy()
        dump('ymixh', ymixT[:, 6, 512:1024], 512, [('ymix', 6)])

        WO = YB
        wost = [V(WO, F32, 8, 256), V(WO + 2048, F32, 8, 256)]
        wobf = [V(WO + 4096, BF16, 8, 256), V(WO + 5120, BF16, 8, 256)]
        S.phase()
        w_out_v = w_out_d.rearrange("(kc p) n -> p kc n", p=128)
        def pre_wo(nj):
            st, wb = wost[nj % 2], wobf[nj % 2]
            S.dma(lambda e, st=st, nj=nj: e.dma_start(out=st, in_=w_out_v[:, :, nj * 256:(nj + 1) * 256]), writes=[('wost', nj % 2)])
            cp('pool', wb, st, [('wost', nj % 2)], [('wobf', nj % 2)])

        pre_wo(0)
        for nj in range(4):
            wb = wobf[nj % 2]
            if nj + 1 < 4:
                pre_wo(nj + 1)
            for n2 in range(2):
                n = nj * 2 + n2
                for tc in range(NTC):
                    b = nb()
                    for kc in range(8):
                        mm(bank(b), wb[:, kc, n2 * 128:(n2 + 1) * 128], ymixT[:, kc, tc * TCW:(tc + 1) * TCW], kc == 0, kc == 7,
                           [('wobf', nj % 2), ('ymix', kc)], [PK(b)])
                    sl = xT[:, n, tc * TCW:(tc + 1) * TCW]
                    tt('dve', sl, sl, bank(b), ALU.add, [PK(b), ('xT', n, tc)], [('xT', n, tc)])
        dump('xmix', xT[:, 3, 512:1024], 512, [('xT', 3, 1)])

        S.phase()
        h2T = V(YB, BF16, 8, T)
        actb = [V(YB + 8192, BF16, 4, T), V(YB + 8192 + 4096, BF16, 4, T)]
        E0 = BASE
        accg = V(E0, F32, T)
        accu = V(E0 + 2048, F32, T)
        wust = [V(E0 + 4096, F32, 8, 256), V(E0 + 6144, F32, 8, 256)]
        wubf = [V(E0 + 8192, BF16, 8, 256), V(E0 + 9216, BF16, 8, 256)]
        wdst = [V(E0 + 10240, F32, 1024), V(E0 + 11264, F32, 1024)]
        wdbf = [V(E0 + 12288, BF16, 4, 1024), V(E0 + 14336, BF16, 4, 1024)]
        sqE = [V(E0 + 16384, BF16, TCW), V(E0 + 16384 + 256, BF16, TCW)]
        rsE = [V(E0 + 16896, F32, TCW), V(E0 + 16896, F32, TCW)]
        assert E0 + 17408 <= R2

        def fm_norm_x(dst, gname, dkey):
            for tc in range(NTC):
                b = nb()
                for kc in range(8):
                    sq = sqE[kc % 2]
                    act(sq, xT[:, kc, tc * TCW:(tc + 1) * TCW], AF.Square, [('xT', kc, tc)], [('sqE', kc % 2)])
                    mm(bank(b), ones, sq, kc == 0, kc == 7, [('sqE', kc % 2), 'cbf'], [PK(b)])
                rs = rsE[0]
                rstd_from_bank(b, D, rs, 128, ('rsE', 0))
                for kc in range(8):
                    stt('dve', dst[:, kc, tc * TCW:(tc + 1) * TCW], xT[:, kc, tc * TCW:(tc + 1) * TCW], vc(gname, kc), rs, ALU.mult, ALU.mult,
                        [('xT', kc, tc), ('rsE', 0), 'vec'], [(dkey, kc, tc)])
        fm_norm_x(h2T, 'norm_ffn', 'h2T')
        dump('h2T', h2T[:, 7, 1536:2048], 512, [('h2T', 7, 3)])
        w_up_v = w_up_d.rearrange("(kc p) n -> p kc n", p=128)
        rounds = [list(range(i, min(i + 4, NFF))) for i in range(0, NFF, 4)]
        pairs = [(r, jj, j) for r in range(len(rounds)) for jj, j in enumerate(rounds[r])]

        def pre_up(pi):
            r, jj, j = pairs[pi]
            s = pi % 2
            st, wb = wust[s], wubf[s]
            S.dma(lambda e, st=st, j=j: e.dma_start(out=st[:, :, 0:128], in_=w_up_v[:, :, j * 128:(j + 1) * 128]), writes=[('wust', s)])
            S.dma(lambda e, st=st, j=j: e.dma_start(out=st[:, :, 128:256], in_=w_up_v[:, :, DFF + j * 128:DFF + (j + 1) * 128]), writes=[('wust', s)])
            cp('act', wb, st, [('wust', s)], [('wubf', s)])

        dctr = [0]

        def pre_down(r, jj):
            j = rounds[r][jj]
            slot = r % 2
            ds = dctr[0] % 2
            dctr[0] += 1
            st = wdst[ds]
            S.dma(lambda e, st=st, j=j: e.dma_start(out=st, in_=w_down_d[j * 128:(j + 1) * 128, :]), writes=[('wdst', ds)])
            cp('act', wdbf[slot][:, jj, :], st, [('wdst', ds)], [('wdbf', slot, jj)])

        def up_pair(pi):
            r, jj, j = pairs[pi]
            slot = r % 2
            s = pi % 2
            wb = wubf[s]
            for gu in range(2):
                for tc in range(NTC):
                    b = gu * 4 + tc
                    for kc in range(8):
                        mm(bank(b), wb[:, kc, gu * 128:(gu + 1) * 128], h2T[:, kc, tc * TCW:(tc + 1) * TCW], kc == 0, kc == 7,
                           [('wubf', s), ('h2T', kc, tc)], [PK(b)])
                psv = ps[:, gu * 2048:(gu + 1) * 2048]
                pk = [PK(gu * 4 + t_) for t_ in range(4)]
                ci = j if gu == 0 else NFF + j
                acc, akey = (accg, 'accg') if gu == 0 else (accu, 'accu')
                conv3(psv, vc('fcw0', ci), vc('fcw1', ci), vc('fcw2', ci), vc('fcb', ci), acc, akey, pk)
                if gu == 0:
                    act(actb[slot][:, jj, :], accg, AF.Silu, ['accg'], [('actb', slot, jj)])
            tt('pool', actb[slot][:, jj, :], actb[slot][:, jj, :], accu, ALU.mult, [('actb', slot, jj), 'accu'], [('actb', slot, jj)])

        def down_round(r):
            slot = r % 2
            nj = len(rounds[r])
            for n in range(8):
                for tc in range(NTC):
                    b = nb()
                    for jj in range(nj):
                        mm(bank(b), wdbf[slot][:, jj, n * 128:(n + 1) * 128], actb[slot][:, jj, tc * TCW:(tc + 1) * TCW], jj == 0, jj == nj - 1,
                           [('wdbf', slot, jj), ('actb', slot, jj)], [PK(b)])
                    sl = xT[:, n, tc * TCW:(tc + 1) * TCW]
                    tt('dve', sl, sl, bank(b), ALU.add, [PK(b), ('xT', n, tc)], [('xT', n, tc)])

        NR = len(rounds)
        NP = len(pairs)
        pre_up(0)
        pi = 0
        for r in range(NR):
            for jj in range(len(rounds[r])):
                if pi + 1 < NP:
                    pre_up(pi + 1)
                if r >= 1 and jj < len(rounds[r - 1]):
                    pre_down(r - 1, jj)
                up_pair(pi)
                pi += 1
            if r >= 1:
                for jj in range(len(rounds[r]), len(rounds[r - 1])):
                    pre_down(r - 1, jj)
                down_round(r - 1)
        S.phase()
        F0 = BASE
        pTb = V(F0, BF16, 2, T)
        pst = V(F0 + 2048, F32, 2, T)
        wpb = V(F0 + 6144, BF16, 2, 1024)
        wpst = V(F0 + 7168, F32, 2, 1024)
        S.dma(lambda e: e.dma_start(out=pst, in_=pT_d.rearrange("(kc p) t -> p kc t", p=128)), writes=['pst'])
        S.dma(lambda e: e.dma_start(out=wpst, in_=w_ple_d.rearrange("(kc p) n -> p kc n", p=128)), writes=['wpst'])
        for jj in range(len(rounds[NR - 1])):
            pre_down(NR - 1, jj)
        cp('pool', pTb, pst, ['pst'], ['pTb'])
        cp('pool', wpb, wpst, ['wpst'], ['wpb'])
        down_round(NR - 1)
        dump('xffn', xT[:, 5, 0:512], 512, [('xT', 5, 0)])

        S.phase()
        xb = V(YB, BF16, 8, T)
        eT = V(YB + 8192, BF16, 8, T)
        wgst = [V(F0 + 9216, F32, 8, 128), V(F0 + 10240, F32, 8, 128)]
        wgbf = [V(F0 + 11264, BF16, 8, 128), V(F0 + 11776, BF16, 8, 128)]
        sqF = [V(F0 + 12288, BF16, T), V(F0 + 13312, BF16, T)]
        rsF = [V(F0 + 14336, F32, TCW), V(F0 + 14848, F32, TCW)]
        sgt = [V(F0 + 15360, F32, TCW), V(F0 + 15872, F32, TCW)]
        assert F0 + 16384 <= R2
        for kc in range(8):
            cp('pool' if kc % 2 else 'act', xb[:, kc, :], xT[:, kc, :], [('xT', kc, t_) for t_ in range(4)], [('xb', kc)])
        for n in range(8):
            for tc in range(NTC):
                b = nb()
                for kc in range(2):
                    mm(bank(b), wpb[:, kc, n * 128:(n + 1) * 128], pTb[:, kc, tc * TCW:(tc + 1) * TCW], kc == 0, kc == 1, ['wpb', 'pTb'], [PK(b)])
                cp('act', eT[:, n, tc * TCW:(tc + 1) * TCW], bank(b), [PK(b)], [('eT', n)])
        fm_norm_inplace(eT, 8, D, 'ple_norm', 'eT', sqF, rsF, 'sqF', 'rsF')
        dump('eT', eT[:, 4, 0:512], 512, [('eT', 4)])
        w_gate_v = w_gate_d.rearrange("(kc p) n -> p kc n", p=128)
        out_toks = []
        def pre_wg(n):
            st, wb = wgst[n % 2], wgbf[n % 2]
            S.dma(lambda e, st=st, n=n: e.dma_start(out=st, in_=w_gate_v[:, :, n * 128:(n + 1) * 128]), writes=[('wgst', n % 2)])
            cp('pool', wb, st, [('wgst', n % 2)], [('wgbf', n % 2)])

        pre_wg(0)
        for n in range(8):
            wb = wgbf[n % 2]
            if n + 1 < 8:
                pre_wg(n + 1)
            for tc in range(NTC):
                b = nb()
                for kc in range(8):
                    mm(bank(b), wb[:, kc, :], xb[:, kc, tc * TCW:(tc + 1) * TCW], kc == 0, kc == 7, [('wgbf', n % 2), ('xb', kc)], [PK(b)])
                sg = sgt[tc % 2]
                act(sg, bank(b), AF.Sigmoid, [PK(b)], [('sg', tc % 2)])
                tt('dve', sg, sg, eT[:, n, tc * TCW:(tc + 1) * TCW], ALU.mult, [('sg', tc % 2), ('eT', n)], [('sg', tc % 2)])
                sl = xT[:, n, tc * TCW:(tc + 1) * TCW]
                tt('dve', sl, sl, sg, ALU.add, [('sg', tc % 2), ('xT', n, tc)], [('xT', n, tc)])
            out_toks.append(S.dma(lambda e, n=n: e.dma_start(out=outT_d[n * 128:(n + 1) * 128, :], in_=xT[:, n, :]),
                                  reads=[('xT', n, t_) for t_ in range(4)], writes=[('out', n)]))
        S.finish(out_toks + [('dma', n) for n in range(max(0, S.n_dma - N_DMA_SEMS), S.n_dma)])
        S.emit(block, sems, dsems)
    return nc


_CONST_CACHE = {}


def _consts():
    if 'c' in _CONST_CACHE:
        return _CONST_CACHE['c']
    bf = ml_dtypes.bfloat16
    c = {}
    cb = np.zeros((128, 736), np.float32)
    cb[:, 0:128] = np.eye(128)
    cb[:, 128:256] = 1.0
    R = np.zeros((96, 96), np.float32)
    for i in range(16):
        R[80 + i, 64 + i] = -1.0
        R[64 + i, 80 + i] = 1.0
    cb[0:96, 256:352] = R
    E = np.zeros((32, 96), np.float32)
    for i in range(32):
        E[i, 64 + i] = 1.0
    cb[0:32, 352:448] = E
    cb[:, 448] = (-1.0) ** np.arange(128)
    cb[0:96, 480:608] = 1.0
    cb[0:96, 608:704] = R
    c['cbf'] = cb.astype(bf)
    c['altrow'] = ((-1.0) ** np.arange(T)).astype(np.float32).reshape(1, T).astype(bf)
    pos = np.arange(T, dtype=np.float32)
    inv_freq = (10000.0 ** (-np.arange(0, 32, 2, dtype=np.float32) / 32)).astype(np.float32)
    ang = pos[None, :] * inv_freq[:, None]
    rope = np.zeros((2, 96, T), np.float32)
    rope[0, 0:64] = 1.0
    rope[0, 64:80] = np.cos(ang)
    rope[0, 80:96] = np.cos(ang)
    rope[1, 64:80] = np.sin(ang)
    rope[1, 80:96] = np.sin(ang)
    c['rope'] = rope
    L = T
    t = np.linspace(0.0, 1.0, L, dtype=np.float32)
    bands = 16
    fb = np.linspace(1e-4, bands - 1, bands, dtype=np.float32)
    w = (2.0 * np.pi * np.arange(L, dtype=np.float32) / L).astype(np.float32)
    a2 = w[:, None] * fb[None, :]
    z = np.concatenate([t[:, None], np.cos(a2), -np.sin(a2)], axis=-1).astype(np.float32)
    c['zT'] = np.ascontiguousarray(z.T)
    min_decay = np.log(1e-2) / 0.3
    max_decay = np.log(1e-2) / 1.5
    deltas = np.abs(np.linspace(min_decay, max_decay, 512, dtype=np.float32))
    c['decay'] = np.exp(-t[:, None] * deltas[None, :]).astype(np.float32)
    idx = np.arange(T, dtype=np.int64)
    prod = (idx[:, None] * idx[None, :]) % 4096
    angd = prod.astype(np.float64) * (2.0 * np.pi / 4096.0)
    C = np.cos(angd).astype(np.float32)
    Sn = np.sin(angd).astype(np.float32)
    c['cinv'] = C.astype(bf)
    c['sinv'] = Sn.astype(bf)
    c['ctf'] = np.ascontiguousarray(c['cinv'].reshape(16, 128, 16, 128).transpose(2, 1, 0, 3)).reshape(16, 128, 2048)
    c['stf'] = np.ascontiguousarray(c['sinv'].reshape(16, 128, 16, 128).transpose(2, 1, 0, 3)).reshape(16, 128, 2048)
    _CONST_CACHE['c'] = c
    return c


def _colpack(v, rows=128):
    v = np.asarray(v, np.float32).reshape(-1)
    k = (v.size + rows - 1) // rows
    out = np.zeros((rows, k), np.float32)
    for j in range(k):
        seg = v[j * rows:(j + 1) * rows]
        out[:seg.size, j] = seg
    return out


def _pack_vecs(inp):
    cols = {}
    cols['norm_mix'] = _colpack(inp['norm_mix'][0])
    cols['q_norm'] = _colpack(inp['q_norm'][0])
    cols['kv_norm'] = _colpack(inp['kv_norm'][0])
    cols['gq'] = _colpack(inp['qk_norm_q'][0])
    cols['gk'] = _colpack(inp['qk_norm_k'][0])
    for i in range(3):
        cols['scw%d' % i] = _colpack(inp['short_conv_w'][0, i])
        cols['fcw%d' % i] = _colpack(inp['ffn_conv_w'][0, i])
    cols['scb'] = _colpack(inp['short_conv_b'][0])
    cols['fcb'] = _colpack(inp['ffn_conv_b'][0])
    cols['hbias'] = _colpack(inp['hyena_bias'][0])
    cols['g_attn'] = _colpack(inp['out_norm_attn'][0])
    cols['g_hy'] = _colpack(inp['out_norm_hyena'][0])
    cols['norm_ffn'] = _colpack(inp['norm_ffn'][0])
    cols['ple_norm'] = _colpack(inp['ple_norm'][0])
    cols['fb1'] = _colpack(inp['filt_b1'][0])
    cols['fb2'] = _colpack(inp['filt_b2'][0])
    cols['fb3'] = _colpack(inp['filt_b3'][0])
    cols['ffr'] = _colpack(inp['filt_freq'][0])
    cols['eps'] = np.full((128, 1), EPS, np.float32)
    wf0 = np.full((128, 1), 2.0 / 4096, np.float32)
    wf0[0, 0] = 1.0 / 4096
    cols['wf0'] = wf0
    cols['wf1'] = np.full((128, 1), 2.0 / 4096, np.float32)
    out = np.zeros((128, NV), np.float32)
    for name, k in VEC_LAYOUT:
        a = cols[name]
        assert a.shape[1] == k, (name, a.shape, k)
        out[:, VCOL[name]:VCOL[name] + k] = a
    return out


def make_in_maps(inp, ncores=8):
    c = _consts()
    f32 = lambda a: np.ascontiguousarray(np.asarray(a, np.float32))
    shared = {
        'w_in': f32(inp['w_in'][0]), 'w_uq': f32(inp['w_uq'][0]), 'w_ukv': f32(inp['w_ukv'][0]),
        'w_out': f32(inp['w_out'][0]), 'w_up': f32(inp['w_up'][0]), 'w_down': f32(inp['w_down'][0]),
        'w_ple': f32(inp['w_ple'][0]), 'w_gate': f32(inp['w_ple_gate'][0]),
        'fw1': f32(inp['filt_w1'][0]), 'fw2': f32(inp['filt_w2'][0]), 'fw3': f32(inp['filt_w3'][0]),
        'fwo': f32(inp['filt_w_out'][0]), 'vecs': _pack_vecs(inp),
    }
    shared.update(c)
    maps = []
    for b in range(ncores):
        m = dict(shared)
        m['xT'] = np.ascontiguousarray(np.asarray(inp['x'][b], np.float32).T)
        m['pT'] = np.ascontiguousarray(np.asarray(inp['p'][0, b], np.float32).T)
        maps.append(m)
    return maps


_PROG = {}


def kernel(**inputs):
    if 'nc' not in _PROG:
        _PROG['nc'] = build_program()
    nc = _PROG['nc']
    maps = make_in_maps(inputs, 8)
    res = run_bass_kernel_spmd(nc, maps, core_ids=list(range(8)))
    out = np.stack([np.asarray(r['outT'], np.float32).T for r in res.results], axis=0)
    return np.ascontiguousarray(out)
```

```python
import numpy as np
import ml_dtypes
from contextlib import ExitStack
import concourse.bass as bass
import concourse.mybir as mybir
from concourse.bass_utils import run_bass_kernel_spmd

F32 = mybir.dt.float32
BF16 = mybir.dt.bfloat16
ALU = mybir.AluOpType
AF = mybir.ActivationFunctionType

ENGS = ['pe', 'act', 'dve', 'pool', 'sp']
N_DMA_SEMS = 24


class Sched:
    def __init__(self, nc):
        self.nc = nc
        self.ops = {e: [] for e in ENGS}
        self.last_w = {}
        self.readers = {}
        self.n_dma = 0
        self.dma_tok_of_slot = {}
        self.frontier = set()
        self.known = set()
        self.stopped = False

    def phase(self):
        fr = set()
        for e in ENGS:
            if self.ops[e]:
                for i in range(len(self.ops[e]) - 1, -1, -1):
                    if self.ops[e][i]['dma'] is None and self.ops[e][i]['fn'] is not None:
                        fr.add((e, i))
                        break
        for n in range(max(0, self.n_dma - N_DMA_SEMS), self.n_dma):
            fr.add(('dma', n))
        self.frontier = fr
        self.known = set()

    def _deps(self, eng, reads, writes):
        deps = set()
        for k in reads:
            t = self.last_w.get(k)
            if t is not None:
                deps.add(t)
        for k in writes:
            t = self.last_w.get(k)
            if t is not None:
                deps.add(t)
            for r in self.readers.get(k, {}).values():
                deps.add(r)
            if k not in self.known:
                deps |= self.frontier
        return deps

    def _commit(self, tok, reads, writes):
        for k in writes:
            self.known.add(k)
            self.last_w[k] = tok
            self.readers[k] = {}
        for k in reads:
            self.readers.setdefault(k, {})[tok if tok[0] == 'dma' else tok[0]] = tok

    def retire(self, keys):
        toks = set()
        for k in keys:
            t = self.last_w.pop(k, None)
            if t is not None:
                toks.add(t)
            for r in self.readers.pop(k, {}).values():
                toks.add(r)
        return toks

    def adopt(self, keys, toks):
        for k in keys:
            d = self.readers.setdefault(k, {})
            for t in toks:
                d[('x', t)] = t

    def op(self, eng, fn, reads=(), writes=()):
        if self.stopped:
            return None
        deps = self._deps(eng, reads, writes)
        idx = len(self.ops[eng])
        tok = (eng, idx)
        if eng == 'pe':
            deps = {d for d in deps if d[0] != 'pe'}
        deps.discard(tok)
        self.ops[eng].append(dict(fn=fn, deps=deps, dma=None))
        self._commit(tok, reads, writes)
        return tok

    def dma(self, fn, reads=(), writes=(), q='sp'):
        if self.stopped:
            return None
        deps = self._deps(q, reads, writes)
        n = self.n_dma
        self.n_dma += 1
        slot = n % N_DMA_SEMS
        prev = self.dma_tok_of_slot.get(slot)
        if prev is not None:
            deps.add(prev)
        tok = ('dma', n)
        self.dma_tok_of_slot[slot] = tok
        self.ops[q].append(dict(fn=fn, deps=deps, dma=n))
        self._commit(tok, reads, writes)
        return tok

    def finish(self, toks, q='sp'):
        if self.stopped:
            return
        self.ops[q].append(dict(fn=None, deps=set(toks), dma=None))

    def emit(self, block, sems, dma_sems):
        if self.stopped:
            return
        nc = self.nc
        mile = {e: {} for e in ENGS}
        needed = {e: set() for e in ENGS}
        for e in ENGS:
            for o in self.ops[e]:
                for d in o['deps']:
                    if d[0] != 'dma':
                        needed[d[0]].add(d[1])
        for e in ENGS:
            c = 0
            for i in range(len(self.ops[e])):
                if i in needed[e]:
                    c += 1
                    mile[e][i] = c
        self.mile = mile

        def run(ename, engine):
            waited = {}
            for i, o in enumerate(self.ops[ename]):
                want = {}
                for d in o['deps']:
                    if d[0] == 'dma':
                        n = d[1]
                        s = ('d', n % N_DMA_SEMS)
                        v = 16 * (n // N_DMA_SEMS + 1)
                    else:
                        s = ('e', d[0])
                        v = mile[d[0]][d[1]]
                    if v > want.get(s, 0):
                        want[s] = v
                for s, v in want.items():
                    if waited.get(s, 0) >= v:
                        continue
                    waited[s] = v
                    sem = dma_sems[s[1]] if s[0] == 'd' else sems[s[1]]
                    engine.wait_ge(sem, v)
                if o['fn'] is None:
                    continue
                ins = o['fn'](engine)
                if o['dma'] is not None:
                    ins.then_inc(dma_sems[o['dma'] % N_DMA_SEMS], 16)
                elif i in mile[ename]:
                    ins.then_inc(sems[ename], 1)

        @block.tensor
        def _(e):
            run('pe', e)

        @block.scalar
        def _(e):
            run('act', e)

        @block.vector
        def _(e):
            run('dve', e)

        @block.gpsimd
        def _(e):
            run('pool', e)

        @block.sync
        def _(e):
            run('sp', e)


T = 2048
NTC = 4
TCW = 512
D = 1024
EPS = 1e-6
DFF = 2816
NFF = 22

ARENA_W = 52480
NDUMMY = 0
B1_BARRIERS = True
B1_MASK = (0, 0, 0, 0, 0)
B1_TARGETED = True

VEC_LAYOUT = [
    ('norm_mix', 8), ('q_norm', 6), ('kv_norm', 2), ('gq', 1), ('gk', 1),
    ('scw0', 12), ('scw1', 12), ('scw2', 12), ('scb', 12), ('hbias', 4),
    ('g_attn', 4), ('g_hy', 4), ('norm_ffn', 8),
    ('fcw0', 44), ('fcw1', 44), ('fcw2', 44), ('fcb', 44), ('ple_norm', 8),
    ('fb1', 1), ('fb2', 1), ('fb3', 1), ('ffr', 1), ('eps', 1), ('wf0', 1), ('wf1', 1),
]
VCOL = {}
_c = 0
for _n, _k in VEC_LAYOUT:
    VCOL[_n] = _c
    _c += _k
NV = _c
NVX = NV + 16


class _Done(Exception):
    pass


def build_program(dbg=None):
    nc = bass.Bass("TRN2", target_bir_lowering=False)

    def din(name, shape, dt=F32):
        return nc.dram_tensor(name, list(shape), dt, kind="ExternalInput").ap()

    xT_d = din("xT", [D, T])
    pT_d = din("pT", [256, T])
    w_in_d = din("w_in", [D, 2592])
    w_uq_d = din("w_uq", [768, 768])
    w_ukv_d = din("w_ukv", [256, 1024])
    w_out_d = din("w_out", [D, D])
    w_up_d = din("w_up", [D, 2 * DFF])
    w_down_d = din("w_down", [DFF, D])
    w_ple_d = din("w_ple", [256, D])
    w_gate_d = din("w_gate", [D, D])
    fw1_d = din("fw1", [33, 64])
    fw2_d = din("fw2", [64, 64])
    fw3_d = din("fw3", [64, 64])
    fwo_d = din("fwo", [64, 1024])
    vecs_d = din("vecs", [128, NV])
    cbf_d = din("cbf", [128, 736], BF16)
    altrow_d = din("altrow", [1, T], BF16)
    rope_d = din("rope", [2, 96, T])
    zT_d = din("zT", [33, T])
    decay_d = din("decay", [T, 512])
    ctf_d = din("ctf", [16, 128, 2048], BF16)
    stf_d = din("stf", [16, 128, 2048], BF16)
    cinv_d = din("cinv", [T, T], BF16)
    sinv_d = din("sinv", [T, T], BF16)
    outT_d = nc.dram_tensor("outT", [D, T], F32, kind="ExternalOutput").ap()
    dbg_d = None
    if dbg is not None:
        dbg_d = nc.dram_tensor("dbg", [128, dbg[1]], F32, kind="ExternalOutput").ap()

    S = Sched(nc)
    with ExitStack() as es:
        arena = es.enter_context(nc.sbuf_tensor("arena", [128, ARENA_W], F32))
        ps = es.enter_context(nc.psum_tensor("ps", [128, 4096], F32))
        sems = {e: es.enter_context(nc.semaphore("s_" + e)) for e in ENGS}
        dsems = [es.enter_context(nc.semaphore("d%d" % i)) for i in range(N_DMA_SEMS)]
        block = es.enter_context(nc.Block())

        def V(off, dt, *shape):
            n = 1
            for s_ in shape:
                n *= s_
            words = n if dt == F32 else (n + 1) // 2
            assert off + words <= ARENA_W, (off, words)
            ap = arena[:, off:off + words]
            if dt == BF16:
                ap = ap.bitcast(BF16)
            if len(shape) == 2:
                ap = ap.rearrange("p (a b) -> p a b", b=shape[1])
            elif len(shape) == 3:
                ap = ap.rearrange("p (a b c) -> p a b c", b=shape[1], c=shape[2])
            return ap

        def bank(b, p=128, n=512, o=0):
            return ps[0:p, b * 512 + o: b * 512 + o + n]

        bctr = [0]

        def nb():
            b = bctr[0] % 8
            bctr[0] += 1
            return b

        def PK(b):
            return ('ps', b)

        cbf = V(0, BF16, 736)
        ident = cbf[:, 0:128]
        ones = cbf[:, 128:256]
        Rrot = cbf[0:96, 256:352]
        Esh = cbf[:, 352:480]
        altc = cbf[:, 448:449]
        ones96 = cbf[:, 480:608]
        Rrot128 = cbf[:, 608:736]
        vec = V(368, F32, NVX)
        o_ = 368 + NVX
        altrow = V(o_, BF16, T)
        o_ += 1024
        ynyq = V(o_, BF16, 512)
        o_ += 256
        assert o_ <= 2048, o_
        BASE = 2048

        def vc(name, i=0, p=128):
            c = VCOL[name] + i
            return vec[0:p, c:c + 1]

        def vx(i, p=128):
            return vec[0:p, NV + i:NV + i + 1]

        for _i in range(NDUMMY):
            S.op('dve', lambda e: e.memset(arena[:, 2040:2044], 0.0), writes=['DUMMY'])
        S.dma(lambda e: e.dma_start(out=cbf, in_=cbf_d), writes=['cbf'])
        S.dma(lambda e: e.dma_start(out=vec[:, 0:NV], in_=vecs_d), writes=['vec'])
        S.dma(lambda e: e.dma_start(out=altrow[0:1, :], in_=altrow_d), writes=['altrow'])

        def mm(out, lhsT, rhs, start, stop, reads, writes):
            S.op('pe', lambda e: e.matmul(out, lhsT=lhsT, rhs=rhs, start=start, stop=stop), reads=reads, writes=writes)

        def act(out, in_, func, reads, writes, scale=1.0, bias=None, accum_out=None):
            kw = {}
            if bias is not None:
                kw['bias'] = bias
            if accum_out is not None:
                kw['accum_out'] = accum_out
            S.op('act', lambda e: e.activation(out=out, in_=in_, func=func, scale=scale, **kw), reads=reads, writes=writes)

        def stt(eng, out, in0, scalar, in1, op0, op1, reads, writes):
            S.op(eng, lambda e: e.scalar_tensor_tensor(out=out, in0=in0, scalar=scalar, in1=in1, op0=op0, op1=op1), reads=reads, writes=writes)

        def tt(eng, out, in0, in1, op, reads, writes):
            S.op(eng, lambda e: e.tensor_tensor(out=out, in0=in0, in1=in1, op=op), reads=reads, writes=writes)

        def ts(eng, out, in0, s1, s2, op0, op1, reads, writes):
            if s2 is None:
                S.op(eng, lambda e: e.tensor_scalar(out=out, in0=in0, scalar1=s1, scalar2=None, op0=op0), reads=reads, writes=writes)
            else:
                S.op(eng, lambda e: e.tensor_scalar(out=out, in0=in0, scalar1=s1, scalar2=s2, op0=op0, op1=op1), reads=reads, writes=writes)

        def cp(eng, out, in_, reads, writes):
            if eng == 'act':
                S.op('act', lambda e: e.copy(out=out, in_=in_), reads=reads, writes=writes)
            else:
                S.op(eng, lambda e: e.tensor_copy(out=out, in_=in_), reads=reads, writes=writes)

        def rstd_from_bank(b, Dn, dst, p, dkey, extra=()):
            act(dst, bank(b, p), AF.Ln, [PK(b), 'vec'] + list(extra), [dkey], scale=1.0 / Dn, bias=vc('eps', 0, p))
            act(dst, dst, AF.Exp, [dkey], [dkey], scale=-0.5)

        dbg_done = [False]

        def dump(tag, ap, ncols, keys, p=128):
            if dbg is None or dbg[0] != tag or dbg_done[0]:
                return
            dbg_done[0] = True
            tmp = V(ARENA_W - dbg[1], F32, dbg[1])
            cp('dve', tmp[0:p, 0:ncols], ap, keys, ['dbgtmp'])
            S.dma(lambda e: e.dma_start(out=dbg_d[0:p, 0:ncols], in_=tmp[0:p, 0:ncols]), reads=['dbgtmp'], writes=['dbgout'])
            finish_program()
            S.stopped = True

        def finish_program(extra=()):
            toks = [('dma', n) for n in range(max(0, S.n_dma - N_DMA_SEMS), S.n_dma)]
            S.finish(toks)
            S.emit(block, sems, dsems)

        uT = V(BASE, BF16, 4, T)
        x0T = V(BASE + 4096, BF16, 4, T)
        R1 = BASE + 8192
        cqT = V(R1, BF16, 6, T)
        ckvT = V(R1 + 6144, BF16, 2, T)
        kpeT = V(R1 + 8192, BF16, T)
        ymixT = V(R1, BF16, 8, T)
        R2 = R1 + 9216

        hT = V(R2, BF16, 8, T)
        XS = R2 + 8192
        xst = [V(XS, F32, 8, TCW), V(XS + 4096, F32, 8, TCW)]
        sqs = [V(XS + 8192, BF16, T), V(XS + 8192 + 1024, BF16, T)]
        rss = [V(XS + 10240, F32, TCW), V(XS + 10240 + 512, F32, TCW)]
        ACC = XS + 11264
        accs = [V(ACC + i * 2048, F32, T) for i in range(6)]
        assert ACC + 12288 <= ARENA_W

        xT_v = xT_d.rearrange("(kc p) t -> p kc t", p=128)
        for tc in range(NTC):
            xs = xst[tc % 2]
            S.dma(lambda e, xs=xs, tc=tc: e.dma_start(out=xs, in_=xT_v[:, :, tc * TCW:(tc + 1) * TCW]), writes=[('xst', tc % 2)])
            b = nb()
            for kc in range(8):
                sq = sqs[kc % 2][:, 0:TCW]
                act(sq, xs[:, kc, :], AF.Square, [('xst', tc % 2)], [('sq', kc % 2)])
                mm(bank(b), ones, sq, kc == 0, kc == 7, [('sq', kc % 2), 'cbf'], [PK(b)])
            rs = rss[tc % 2]
            rstd_from_bank(b, D, rs, 128, ('rs', tc % 2))
            for kc in range(8):
                stt('dve', hT[:, kc, tc * TCW:(tc + 1) * TCW], xs[:, kc, :], vc('norm_mix', kc), rs, ALU.mult, ALU.mult,
                    [('xst', tc % 2), ('rs', tc % 2), 'vec'], [('hT', kc, tc)])
        dump('hT', hT[:, 0, 0:512], 512, [('hT', 0, 0)])

        WST = XS
        wst = [V(WST, F32, 8, 256), V(WST + 2048, F32, 8, 256)]
        wbf = [V(WST + 4096, BF16, 8, 256), V(WST + 5120, BF16, 8, 256)]
        S.phase()
        S.op('pool', lambda e: e.memset(kpeT, 0.0), writes=['kpe'])
        w_in_v = w_in_d.rearrange("(kc p) n -> p kc n", p=128)
        jobs = []
        for i in range(3):
            jobs.append(([(i * 256, 256)], [('cq', 2 * i), ('cq', 2 * i + 1)]))
        jobs.append(([(768, 256)], [('ckv', 0), ('ckv', 1)]))
        jobs.append(([(1024, 32)], [('kpe', 0)]))
        for i in range(2):
            jobs.append(([(1056 + i * 256, 256)], [('x0', 2 * i), ('x0', 2 * i + 1)]))
        for j in range(4):
            jobs.append(([(1056 + 512 + j * 128, 128), (1056 + 1024 + j * 128, 128)], [('x1', j), ('vh', j)]))
        half = [0]

        def conv3(psv, w0, w1, w2, bb, acc, akey, rkeys, p=128):
            act(acc, psv, AF.Identity, rkeys + ['vec'], [akey], scale=w1, bias=bb)
            stt('dve', acc[:, 1:T], psv[:, 0:T - 1], w0, acc[:, 1:T], ALU.mult, ALU.add, rkeys + [akey, 'vec'], [akey])
            stt('dve', acc[:, 0:T - 1], psv[:, 1:T], w2, acc[:, 0:T - 1], ALU.mult, ALU.add, rkeys + [akey, 'vec'], [akey])

        def pre_job(ji):
            segs, _ = jobs[ji]
            st = wst[ji % 2]
            wb = wbf[ji % 2]
            off = 0
            for (c0, n) in segs:
                S.dma(lambda e, st=st, off=off, c0=c0, n=n: e.dma_start(out=st[:, :, off:off + n], in_=w_in_v[:, :, c0:c0 + n]),
                      writes=[('wst', ji % 2)])
                off += n
            cp('act', wb[:, :, 0:off], st[:, :, 0:off], [('wst', ji % 2)], [('wbf', ji % 2)])

        pre_job(0)
        for ji, (segs, chunks) in enumerate(jobs):
            wb = wbf[ji % 2]
            if ji + 1 < len(jobs):
                pre_job(ji + 1)
            o = 0
            for (kind, idx) in chunks:
                M = 32 if kind == 'kpe' else 128
                h = half[0] % 2
                half[0] += 1
                for tc in range(NTC):
                    b = h * 4 + tc
                    for kc in range(8):
                        mm(bank(b, M), wb[:, kc, o:o + M], hT[:, kc, tc * TCW:(tc + 1) * TCW], kc == 0, kc == 7,
                           [('wbf', ji % 2), ('hT', kc, tc)], [PK(b)])
                o += M
                psv = ps[0:M, h * 2048:(h + 1) * 2048]
                pk = [PK(h * 4 + t_) for t_ in range(4)]
                if kind == 'cq':
                    cp('act', cqT[:, idx, :], psv, pk, [('cq', idx)])
                elif kind == 'ckv':
                    cp('act', ckvT[:, idx, :], psv, pk, [('ckv', idx)])
                elif kind == 'kpe':
                    cp('act', kpeT[0:32, :], psv, pk + ['kpe'], ['kpe'])
                else:
                    wi = {'x0': 0, 'x1': 4, 'vh': 8}[kind] + idx
                    ai = {'x0': 0, 'x1': 1, 'vh': 2}[kind]
                    sl_ = 3 * (idx % 2)
                    conv3(psv, vc('scw0', wi), vc('scw1', wi), vc('scw2', wi), vc('scb', wi), accs[sl_ + ai], ('acc', sl_ + ai), pk)
                    if kind == 'x0':
                        cp('act', x0T[:, idx, :], accs[sl_], [('acc', sl_)], [('x0T', idx)])
                    elif kind == 'vh':
                        tt('pool', uT[:, idx, :], accs[sl_ + 1], accs[sl_ + 2], ALU.mult, [('acc', sl_ + 1), ('acc', sl_ + 2)], [('uT', idx)])
        dump('cq', cqT[:, 0, 0:512], 512, [('cq', 0)])
        dump('kpe', kpeT[0:32, 0:512], 512, ['kpe'], p=32)
        dump('x0', x0T[:, 1, 0:512], 512, [('x0T', 1)])
        dump('u', uT[:, 3, 1536:2048], 512, [('uT', 3)])

        S.phase()
        qhat = V(R2, BF16, 8, T)
        khat = V(R2 + 8192, BF16, 8, T)
        VA = R2 + 16384
        Vaug = V(VA, BF16, 16, 8, 66)
        RP = VA + 4224
        ropeC = V(RP, F32, T)
        ropeS = V(RP + 2048, F32, T)
        WQ = RP + 4096
        wuq = V(WQ, BF16, 6, 800)
        wkn = V(WQ + 2400, BF16, 2, 800)
        wv = V(WQ + 3200, BF16, 2, 8, 64)
        TM = WQ + 3712
        wstg = V(TM, F32, 3, 768)
        assert TM + 4608 <= ARENA_W, TM
        ropeCS = V(RP, F32, 2 * T)
        S.op('pool', lambda e: e.memset(ropeCS, 0.0), writes=['ropeC', 'ropeS'])
        S.dma(lambda e: e.dma_start(out=ropeC[0:96, :], in_=rope_d[0]), writes=['ropeC'])
        S.dma(lambda e: e.dma_start(out=ropeS[0:96, :], in_=rope_d[1]), writes=['ropeS'])
        w_uq_v = w_uq_d.rearrange("(kc p) n -> p kc n", p=128)
        S.op('pool', lambda e: e.memset(wuq[:, :, 768:800], 0.0), writes=[('wuq', 0), ('wuq', 1)])
        for hh in range(2):
            S.dma(lambda e, hh=hh: e.dma_start(out=wstg, in_=w_uq_v[:, hh * 3:(hh + 1) * 3, :]), writes=['wstg'])
            cp('dve', wuq[:, hh * 3:(hh + 1) * 3, 0:768], wstg, ['wstg', ('wuq', hh)], [('wuq', hh)])
        wstg2 = V(TM, F32, 2, 8, 128)
        S.dma(lambda e: e.dma_start(out=wstg2, in_=w_ukv_d.rearrange("(kc p) (h c) -> p kc h c", p=128, c=128)), reads=[], writes=['wstg'])
        S.op('pool', lambda e: e.memset(wkn, 0.0), writes=['wkn'])
        for kc in range(2):
            cp('dve', wkn[:, kc, 0:768].rearrange("p (h c) -> p h c", c=96)[:, :, 0:64], wstg2[:, kc, :, 0:64], ['wstg', 'wkn'], ['wkn'])
            cp('dve', wv[:, kc, :, :], wstg2[:, kc, :, 64:128], ['wstg'], ['wv'])
        sqB = [V(TM, BF16, T), V(TM + 1024, BF16, T)]
        rsB = [V(TM + 2048, F32, TCW), V(TM + 2560, F32, TCW)]
        qbB = [V(TM + 3072, BF16, TCW), V(TM + 3328, BF16, TCW)]
        t1B = V(TM + 3584, F32, TCW)
        t2B = V(TM + 4096, F32, TCW)

        def fm_norm_inplace(buf, nch, Dn, gname, bkey, sqb, rsb, sqk, rsk, extra_reads=()):
            bs = [nb() for _ in range(4)]
            for kc in range(nch):
                sq = sqb[kc % 2]
                tt('dve', sq, buf[:, kc, :], buf[:, kc, :], ALU.mult, [(bkey, kc)] + list(extra_reads), [(sqk, kc % 2)])
                for tc in range(NTC):
                    mm(bank(bs[tc]), ones, sq[:, tc * TCW:(tc + 1) * TCW], kc == 0, kc == nch - 1, [(sqk, kc % 2), 'cbf'], [PK(bs[tc])])
            for tc in range(NTC):
                rs = rsb[tc % 2]
                rstd_from_bank(bs[tc], Dn, rs, 128, (rsk, tc % 2))
                for kc in range(nch):
                    sl = buf[:, kc, tc * TCW:(tc + 1) * TCW]
                    stt('dve', sl, sl, vc(gname, kc), rs, ALU.mult, ALU.mult, [(bkey, kc), (rsk, tc % 2), 'vec'], [(bkey, kc)])

        S.phase()
        fm_norm_inplace(cqT, 6, 768, 'q_norm', 'cq', sqB, rsB, 'sqB', 'rsB')
        fm_norm_inplace(ckvT, 2, 256, 'kv_norm', 'ckv', sqB, rsB, 'sqB', 'rsB')
        dump('cqn', cqT[:, 5, 0:512], 512, [('cq', 5)])

        VaugF = V(VA, BF16, 16 * 8 * 66)
        wv2 = V(WQ + 3200, BF16, 2, 512)
        S.op('pool', lambda e: e.memset(VaugF, 1.0), writes=['Vones'] + [('V', t_) for t_ in range(16)])
        dump('Vm', VaugF[:, 0:512], 512, ['Vones'])
        for ttile in range(16):
            b = nb()
            for kc in range(2):
                mm(bank(b), ckvT[:, kc, ttile * 128:(ttile + 1) * 128], wv2[:, kc, :], kc == 0, kc == 1,
                   [('ckv', kc), 'wv'], [PK(b)])
            dump('Vp', bank(b), 512, [PK(b)])
            cp('dve', Vaug[:, ttile, :, 0:64], bank(b).rearrange("p (h c) -> p h c", c=64), [PK(b), ('V', ttile)], [('V', ttile)])

        free_banks = list(range(8))

        def fb():
            return free_banks.pop(0)

        def stage_a1(item):
            kind, h, tc = item
            b = fb()
            if kind == 'q':
                for kc in range(6):
                    mm(bank(b), wuq[:, kc, h * 96:h * 96 + 128], cqT[:, kc, tc * TCW:(tc + 1) * TCW], kc == 0, kc == 5,
                       [('wuq', kc // 3), ('cq', kc)], [PK(b)])
            else:
                for kc in range(2):
                    mm(bank(b), wkn[:, kc, h * 96:h * 96 + 128], ckvT[:, kc, tc * TCW:(tc + 1) * TCW], kc == 0, False, ['wkn', ('ckv', kc)], [PK(b)])
                mm(bank(b), Esh, kpeT[:, tc * TCW:(tc + 1) * TCW], False, True, ['cbf', 'kpe'], [PK(b)])
            return b

        def stage_a2(item, idx, b):
            kind, h, tc = item
            gname = 'gq' if kind == 'q' else 'gk'
            raw = bank(b)
            qb = qbB[idx % 2]
            act(qb, raw, AF.Copy, [PK(b), 'vec'], [('qb', idx % 2)], scale=vc(gname))
            sq = sqB[idx % 2][:, 0:TCW]
            act(sq, raw, AF.Square, [PK(b)], [('sqB', idx % 2)])
            b1 = fb()
            mm(bank(b1), ones96, sq, True, True, [('sqB', idx % 2), 'cbf'], [PK(b1)])
            b2 = fb()
            mm(bank(b2), Rrot128, qb, True, True, [('qb', idx % 2), 'cbf'], [PK(b2)])
            return (b, b1, b2, gname)

        def stage_b(item, idx, st):
            kind, h, tc = item
            b, b1, b2, gname = st
            raw = bank(b)
            dstbuf, dname = (qhat, 'qhat') if kind == 'q' else (khat, 'khat')
            dst = dstbuf[:, h, tc * TCW:(tc + 1) * TCW]
            rs = rsB[idx % 2]
            rstd_from_bank(b1, 96, rs, 128, ('rsB', idx % 2), extra=[PK(b2)])
            csl = ropeC[:, tc * TCW:(tc + 1) * TCW]
            ssl = ropeS[:, tc * TCW:(tc + 1) * TCW]
            t1, t2 = t1Bs[idx % 2], t2Bs[idx % 2]
            k1, k2 = ('t1B', idx % 2), ('t2B', idx % 2)
            stt('dve', t1, raw, vc(gname), csl, ALU.mult, ALU.mult, [PK(b), 'ropeC', 'vec', PK(b2)], [k1])
            tt('dve', t2, bank(b2), ssl, ALU.mult, [PK(b2), 'ropeS'], [k2])
            tt('dve', t1, t1, t2, ALU.add, [k1, k2], [k1])
            tt('pool', dst, t1, rs, ALU.mult, [k1, ('rsB', idx % 2)], [(dname, h, tc)])
            free_banks.extend([b, b1, b2])

        t1Bs = [t1B, V(TM + 256, F32, TCW)]
        t2Bs = [t2B, V(TM + 1024 + 256, F32, TCW)]
        S.phase()
        items = []
        for h in range(8):
            for tc in range(NTC):
                items.append(('q', h, tc))
                items.append(('k', h, tc))
        NI = len(items)
        raws = {0: stage_a1(items[0]), 1: stage_a1(items[1])}
        sts = {0: stage_a2(items[0], 0, raws[0])}
        for i in range(NI):
            if i + 2 < NI:
                raws[i + 2] = stage_a1(items[i + 2])
            if i + 1 < NI:
                sts[i + 1] = stage_a2(items[i + 1], i + 1, raws[i + 1])
            stage_b(items[i], i, sts[i])
            dump('c%d' % i, qhat[0:96, 0, 0:512], 512, [('qhat', 0, 0)], p=96)
        dump('khat', khat[0:96, 2, 512:1024], 512, [('khat', 2, 1)], p=96)
        dump('qhat', qhat[0:96, 7, 1536:2048], 512, [('qhat', 7, 3)], p=96)

        S.phase()
        PTB = RP
        NPT = 6
        pts = [V(PTB + i * 256, BF16, TCW) for i in range(NPT)]
        YAT = PTB + NPT * 256
        yat = V(YAT, F32, 16, 512)
        SM = YAT + 8192
        rcs = [V(SM + i * 4, F32, 4) for i in range(2)]
        ssqc = V(SM + 8, F32, 16)
        rsc = V(SM + 24, F32, 16)
        ynb = [V(SM + 40, BF16, 512), V(SM + 40 + 256, BF16, 512)]
        assert SM + 40 + 512 <= ARENA_W
        SCALE = 96.0 ** -0.5
        pt2 = [V(PTB + i * 512, BF16, 2 * TCW) for i in range(3)]
        pti = 0
        gi = 0
        sctr = [0]
        for h in range(8):
            for qc in range(NTC):
                gi += 1
                slot_of = {}

                def s_pair(kp, h=h, qc=qc):
                    slot = sctr[0] % 2
                    sctr[0] += 1
                    slot_of[kp] = slot
                    for u in range(2):
                        kt = 2 * kp + u
                        b = 2 * slot + u
                        mm(bank(b), khat[0:96, h, kt * 128:(kt + 1) * 128], qhat[0:96, h, qc * TCW:(qc + 1) * TCW], True, True,
                           [('khat', h, kt // 4), ('qhat', h, qc)], [PK(b)])
                s_pair(0)
                for kp in range(8):
                    if kp + 1 < 8:
                        s_pair(kp + 1)
                    slot = slot_of[kp]
                    pt = pt2[pti % 3]
                    pk = ('pt', pti % 3)
                    pti += 1
                    act(pt, ps[:, (2 * slot) * 512:(2 * slot + 2) * 512], AF.Exp, [PK(2 * slot), PK(2 * slot + 1)], [pk], scale=SCALE)
                    for u in range(2):
                        kt = 2 * kp + u
                        for j in range(4):
                            mm(bank(4 + j, 128, 65), pt[:, u * TCW + j * 128:u * TCW + (j + 1) * 128], Vaug[:, kt, h, 0:65], kt == 0, kt == 15,
                               [pk, ('V', kt), 'Vones'], [PK(4 + j)])
                rc = rcs[gi % 2]
                for j in range(4):
                    S.op('dve', lambda e, rc=rc, j=j: e.reciprocal(out=rc[:, j:j + 1], in_=bank(4 + j, 128, 1, 64)), reads=[PK(4 + j)], writes=[('rc', gi % 2, j)])
                    ts('dve', yat[:, qc * 4 + j, h * 64:(h + 1) * 64], bank(4 + j, 128, 64), rc[:, j:j + 1], None, ALU.mult, None,
                       [PK(4 + j), ('rc', gi % 2, j)], [('yat', qc * 4 + j)])
        dump('yat', yat[:, 5, :], 512, [('yat', 5)])

        S.phase()
        for qt in range(16):
            act(ynb[qt % 2], yat[:, qt, :], AF.Square, [('yat', qt)], [('ynj', qt % 2), ('ssq', qt)], accum_out=ssqc[:, qt:qt + 1])
        act(rsc[:, 0:16], ssqc[:, 0:16], AF.Ln, [('ssq', q_) for q_ in range(16)] + ['vec'], ['rsc'], scale=1.0 / 512, bias=vc('eps'))
        act(rsc[:, 0:16], rsc[:, 0:16], AF.Exp, ['rsc'], ['rsc'], scale=-0.5)
        ynb3 = [V(SM + 552 + i * 256, BF16, 512) for i in range(3)]
        for qt in range(16):
            yn = ynb3[qt % 3]
            ts('dve', yn, yat[:, qt, :], rsc[:, qt:qt + 1], None, ALU.mult, None, [('yat', qt), 'rsc'], [('yn3', qt % 3)])
            b = nb()
            for c in range(4):
                mm(bank(b, 128, 128, c * 128), yn[:, c * 128:(c + 1) * 128], ident, True, True, [('yn3', qt % 3), 'cbf'], [PK(b)])
            for c in range(4):
                act(ymixT[:, c, qt * 128:(qt + 1) * 128], bank(b, 128, 128, c * 128), AF.Copy, [PK(b), 'vec'], [('ymix', c)],
                    scale=vc('g_attn', c))
        dump('ymixa', ymixT[:, 2, 0:512], 512, [('ymix', 2)])

        S.phase()
        utok = V(R2, BF16, 16, 512)
        hsum = V(R2 + 4096, BF16, 16, 512)
        hdiff = V(R2 + 8192, BF16, 16, 512)
        FT = R2 + 12288
        zT = V(FT, F32, T)
        hA = V(FT + 2048, F32, T)
        YB = R2 + 16384
        Ybuf = V(YB, BF16, 32, 512)
        TB = YB + 8192
        hBf = V(TB, F32, T)
        s4b = V(TB + 2048, F32, T)
        fws = V(TB + 4096, F32, 64 * 3 + 1024)
        dcs = [V(TB + 4096 + 1216, F32, 512), V(TB + 4096 + 1216 + 512, F32, 512)]
        hfb = [V(TB + 4096 + 2240 + i * 512, F32, 512) for i in range(4)]
        assert TB + 4096 + 2240 + 2048 <= ARENA_W, TB
        S.dma(lambda e: e.dma_start(out=zT[0:33, :], in_=zT_d), writes=['zT'])
        S.dma(lambda e: e.dma_start(out=fws[0:33, 0:64], in_=fw1_d), writes=['fw1'])
        S.dma(lambda e: e.dma_start(out=fws[0:64, 64:128], in_=fw2_d), writes=['fw2'])
        S.dma(lambda e: e.dma_start(out=fws[0:64, 128:192], in_=fw3_d), writes=['fw3'])
        S.dma(lambda e: e.dma_start(out=fws[0:64, 192:1216], in_=fwo_d), writes=['fwo'])
        ts('dve', vx(0, 64), vc('ffr', 0, 64), 0.5, None, ALU.mult, None, ['vec'], ['vx'])
        ts('dve', vx(1, 64), vc('ffr', 0, 64), 0.25, None, ALU.mult, None, ['vec', 'vx'], ['vx'])
        for l in range(3):
            tt('dve', vx(2 + l, 64), vx(0, 64), vc('fb%d' % (l + 1), 0, 64), ALU.mult, ['vec', 'vx'], ['vx'])
            tt('dve', vx(5 + l, 64), vx(1, 64), vc('fb%d' % (l + 1), 0, 64), ALU.mult, ['vec', 'vx'], ['vx'])
        prev, pkey, Kp = zT, 'zT', 33
        outs = [hA, hBf, hA]
        for l in range(3):
            wl = fws[0:Kp, l * 64:(l + 1) * 64]
            ho = outs[l]
            hk = 'hA' if ho is hA else 'hB'
            for tc in range(NTC):
                b = nb()
                rk = ['zT'] if l == 0 else [(pkey, tc)]
                mm(bank(b, 64), wl, prev[0:Kp, tc * TCW:(tc + 1) * TCW], True, True, ['fw%d' % (l + 1)] + rk, [PK(b)])
                sl = slice(tc * TCW, (tc + 1) * TCW)
                act(ho[0:64, sl], bank(b, 64), AF.Sin, [PK(b), 'vx'], [(hk, tc)], scale=vx(0, 64), bias=vx(2 + l, 64))
                act(s4b[0:64, sl], bank(b, 64), AF.Sin, [PK(b), 'vx'], [('s4', tc)], scale=vx(1, 64), bias=vx(5 + l, 64))
                tt('dve', s4b[0:64, sl], s4b[0:64, sl], s4b[0:64, sl], ALU.mult, [('s4', tc)], [('s4', tc)])
                ts('dve', s4b[0:64, sl], s4b[0:64, sl], -2.0, 1.0, ALU.mult, ALU.add, [('s4', tc)], [('s4', tc)])
                stt('dve', ho[0:64, sl], ho[0:64, sl], 2.0, s4b[0:64, sl], ALU.mult, ALU.mult, [(hk, tc), ('s4', tc)], [(hk, tc)])
            prev, pkey, Kp = ho, hk, 64
        dump('G', hA[0:64, 0:512], 512, [('hA', 0)], p=64)
        for jt in range(16):
            dc = dcs[jt % 2]
            S.dma(lambda e, dc=dc, jt=jt: e.dma_start(out=dc, in_=decay_d[jt * 128:(jt + 1) * 128, :]), writes=[('dc', jt % 2)])
            bf_, bb_ = nb(), nb()
            gk = [('hA', jt // 4)]
            mm(bank(bf_), hA[0:64, jt * 128:(jt + 1) * 128], fws[0:64, 192:192 + 512], True, True, gk + ['fwo'], [PK(bf_)])
            mm(bank(bb_), hA[0:64, jt * 128:(jt + 1) * 128], fws[0:64, 192 + 512:192 + 1024], True, True, gk + ['fwo'], [PK(bb_)])
            hs = 2 * (jt % 2)
            hf_, hb_ = hfb[hs], hfb[hs + 1]
            kf_, kb_ = ('hf', hs), ('hf', hs + 1)
            tt('dve', hf_, bank(bf_), dc, ALU.mult, [PK(bf_), ('dc', jt % 2)], [kf_])
            tt('dve', hb_, bank(bb_), dc, ALU.mult, [PK(bb_), ('dc', jt % 2)], [kb_])
            if jt == 0:
                S.op('dve', lambda e, hb_=hb_: e.memset(hb_[0:1, :], 0.0), reads=[kb_], writes=[kb_])
            tt('pool', hsum[:, jt, :], hf_, hb_, ALU.add, [kf_, kb_], [('hsum', jt)])
            tt('pool', hdiff[:, jt, :], hf_, hb_, ALU.subtract, [kf_, kb_], [('hdiff', jt)])
        dump('hsum', hsum[:, 3, :], 512, [('hsum', 3)])

        S.phase()
        ctab = [V(TB, BF16, 16, 128), V(TB + 1024, BF16, 16, 128)]
        stab = [V(TB + 2048, BF16, 16, 128), V(TB + 3072, BF16, 16, 128)]
        KS = TB + 4096
        kcs = V(KS, F32, 512)
        kss = V(KS + 512, F32, 512)
        tq = [V(KS + 1024 + i * 512, F32, 512) for i in range(4)]
        nyk = V(KS + 3072, F32, 512)
        assert KS + 3584 <= ARENA_W
        for tk in range(16):
            b = nb()
            for c in range(4):
                mm(bank(b, 128, 128, c * 128), uT[:, c, tk * 128:(tk + 1) * 128], ident, True, True, [('uT', c), 'cbf'], [PK(b)])
            cp('act', utok[:, tk, :], bank(b), [PK(b)], [('utok', tk)])
        dump('utok', utok[:, 7, :], 512, [('utok', 7)])
        bu, bk = nb(), nb()
        for tk in range(16):
            mm(bank(bu, 1), altc, utok[:, tk, :], tk == 0, tk == 15, ['cbf', ('utok', tk)], [PK(bu)])
        for tk in range(16):
            mm(bank(bk, 1), altc, hsum[:, tk, :], tk == 0, tk == 15, ['cbf', ('hsum', tk)], [PK(bk)])
        cp('act', nyk[0:1, :], bank(bk, 1), [PK(bk)], ['nyk'])
        stt('dve', ynyq[0:1, :], bank(bu, 1), 1.0 / 4096, nyk[0:1, :], ALU.mult, ALU.mult, [PK(bu), 'nyk'], ['ynyq'])
        for m in range(16):
            ct, st_ = ctab[m % 2], stab[m % 2]
            S.dma(lambda e, ct=ct, m=m: e.dma_start(out=ct, in_=ctf_d[m].rearrange("p (k f) -> p k f", f=128)), writes=[('ctab', m % 2)])
            S.dma(lambda e, st_=st_, m=m: e.dma_start(out=st_, in_=stf_d[m].rearrange("p (k f) -> p k f", f=128)), writes=[('stab', m % 2)])
            hb_ = (m % 2) * 4
            b_uc, b_kc, b_us, b_ks = hb_, hb_ + 1, hb_ + 2, hb_ + 3
            for k in range(16):
                mm(bank(b_uc), ct[:, k, :], utok[:, k, :], k == 0, k == 15, [('ctab', m % 2), ('utok', k)], [PK(b_uc)])
                mm(bank(b_kc), ct[:, k, :], hsum[:, k, :], k == 0, k == 15, [('ctab', m % 2), ('hsum', k)], [PK(b_kc)])
            for k in range(16):
                mm(bank(b_us), st_[:, k, :], utok[:, k, :], k == 0, k == 15, [('stab', m % 2), ('utok', k)], [PK(b_us)])
                mm(bank(b_ks), st_[:, k, :], hdiff[:, k, :], k == 0, k == 15, [('stab', m % 2), ('hdiff', k)], [PK(b_ks)])
            wfc = vc('wf0') if m == 0 else vc('wf1')
            act(kcs, bank(b_kc), AF.Copy, [PK(b_kc), 'vec'], ['kcs'], scale=wfc)
            act(kss, bank(b_ks), AF.Copy, [PK(b_ks), 'vec'], ['kss'], scale=wfc)
            tt('dve', tq[0], bank(b_uc), kcs, ALU.mult, [PK(b_uc), 'kcs'], ['tq0'])
            tt('dve', tq[1], bank(b_us), kss, ALU.mult, [PK(b_us), 'kss'], ['tq1'])
            tt('pool', Ybuf[:, m, :], tq[0], tq[1], ALU.subtract, ['tq0', 'tq1'], [('Y', m)])
            tt('dve', tq[2], bank(b_uc), kss, ALU.mult, [PK(b_uc), 'kss'], ['tq2'])
            tt('dve', tq[3], bank(b_us), kcs, ALU.mult, [PK(b_us), 'kcs'], ['tq3'])
            tt('pool', Ybuf[:, 16 + m, :], tq[2], tq[3], ALU.add, ['tq2', 'tq3'], [('Y', 16 + m)])
        dump('Y', Ybuf[:, 1, :], 512, [('Y', 1)])

        S.phase()
        itab = [V(TB + i * 1024, BF16, T) for i in range(4)]
        ytmp = V(TB + 4096, F32, T)
        assert TB + 6144 <= ARENA_W
        xT = V(R2, F32, 8, T)
        for sw in range(2):
            for fi in range(32):
                it = itab[fi % 4]
                src = cinv_d if fi < 16 else sinv_d
                r0 = (fi % 16) * 128
                S.dma(lambda e, it=it, src=src, r0=r0: e.dma_start(out=it, in_=src[r0:r0 + 128, :]), writes=[('itab', fi % 4)])
                for ci in range(2):
                    c = sw * 2 + ci
                    for tq_ in range(4):
                        b = ci * 4 + tq_
                        mm(bank(b), Ybuf[:, fi, c * 128:(c + 1) * 128], it[:, tq_ * TCW:(tq_ + 1) * TCW], fi == 0, False,
                           [('Y', fi), ('itab', fi % 4)], [PK(b)])
            for ci in range(2):
                c = sw * 2 + ci
                for tq_ in range(4):
                    b = ci * 4 + tq_
                    mm(bank(b), ynyq[0:1, c * 128:(c + 1) * 128], altrow[0:1, tq_ * TCW:(tq_ + 1) * TCW], False, True,
                       ['ynyq', 'altrow'], [PK(b)])
                psv = ps[:, ci * 2048:(ci + 1) * 2048]
                pk = [PK(ci * 4 + t_) for t_ in range(4)]
                stt('dve', ytmp, uT[:, c, :], vc('hbias', c), psv, ALU.mult, ALU.add, pk + [('uT', c), 'vec'], ['ytmp'])
                tt('pool', ymixT[:, 4 + c, :], ytmp, x0T[:, c, :], ALU.mult, ['ytmp', ('x0T', c)], [('ymix', 4 + c)])
        dump('yhy', ymixT[:, 5, 0:512], 512, [('ymix', 5)])
        for g in range(4):
            S.dma(lambda e, g=g: e.dma_start(out=xT[:, 2 * g:2 * g + 2, :], in_=xT_v[:, 2 * g:2 * g + 2, :]), writes=[('xT', 2 * g, t_) for t_ in range(4)] + [('xT', 2 * g + 1, t_) for t_ in range(4)])
        S.phase()
        sqD = [V(TB, BF16, T), V(TB + 1024, BF16, T)]
        rsD = [V(TB + 2048, F32, TCW), V(TB + 2560, F32, TCW)]

        class _Shift:
            def __init__(self, buf, o):
                self.buf, self.o = buf, o

            def __getitem__(self, idx):
                return self.buf[idx[0], idx[1] + self.o, idx[2]]

        def fm_norm_hy():
            bs = [nb() for _ in range(4)]
            for kc in range(4):
                sq = sqD[kc % 2]
                tt('dve', sq, ymixT[:, 4 + kc, :], ymixT[:, 4 + kc, :], ALU.mult, [('ymix', 4 + kc)], [('sqD', kc % 2)])
                for tc in range(NTC):
                    mm(bank(bs[tc]), ones, sq[:, tc * TCW:(tc + 1) * TCW], kc == 0, kc == 3, [('sqD', kc % 2), 'cbf'], [PK(bs[tc])])
            for tc in range(NTC):
                rs = rsD[tc % 2]
                rstd_from_bank(bs[tc], 512, rs, 128, ('rsD', tc % 2))
                for kc in range(4):
                    sl = ymixT[:, 4 + kc, tc * TCW:(tc + 1) * TCW]
                    stt('dve', sl, sl, vc('g_hy', kc), rs, ALU.mult, ALU.mult, [('ymix', 4 + kc), ('rsD', tc % 2), 'vec'], [('ymix', 4 + kc)])
        fm_norm_hy()
        dump('ymixh', ymixT[:, 6, 512:1024], 512, [('ymix', 6)])

        WO = YB
        wost = [V(WO, F32, 8, 256), V(WO + 2048, F32, 8, 256)]
        wobf = [V(WO + 4096, BF16, 8, 256), V(WO + 5120, BF16, 8, 256)]
        S.phase()
        w_out_v = w_out_d.rearrange("(kc p) n -> p kc n", p=128)
        def pre_wo(nj):
            st, wb = wost[nj % 2], wobf[nj % 2]
            S.dma(lambda e, st=st, nj=nj: e.dma_start(out=st, in_=w_out_v[:, :, nj * 256:(nj + 1) * 256]), writes=[('wost', nj % 2)])
            cp('pool', wb, st, [('wost', nj % 2)], [('wobf', nj % 2)])

        pre_wo(0)
        for nj in range(4):
            wb = wobf[nj % 2]
            if nj + 1 < 4:
                pre_wo(nj + 1)
            for n2 in range(2):
                n = nj * 2 + n2
                for tc in range(NTC):
                    b = nb()
                    for kc in range(8):
                        mm(bank(b), wb[:, kc, n2 * 128:(n2 + 1) * 128], ymixT[:, kc, tc * TCW:(tc + 1) * TCW], kc == 0, kc == 7,
                           [('wobf', nj % 2), ('ymix', kc)], [PK(b)])
                    sl = xT[:, n, tc * TCW:(tc + 1) * TCW]
                    tt('dve', sl, sl, bank(b), ALU.add, [PK(b), ('xT', n, tc)], [('xT', n, tc)])
        dump('xmix', xT[:, 3, 512:1024], 512, [('xT', 3, 1)])

        S.phase()
        h2T = V(YB, BF16, 8, T)
        actb = [V(YB + 8192, BF16, 4, T), V(YB + 8192 + 4096, BF16, 4, T)]
        E0 = BASE
        accg = V(E0, F32, T)
        accu = V(E0 + 2048, F32, T)
        wust = [V(E0 + 4096, F32, 8, 256), V(E0 + 6144, F32, 8, 256)]
        wubf = [V(E0 + 8192, BF16, 8, 256), V(E0 + 9216, BF16, 8, 256)]
        wdst = [V(E0 + 10240, F32, 1024), V(E0 + 11264, F32, 1024)]
        wdbf = [V(E0 + 12288, BF16, 4, 1024), V(E0 + 14336, BF16, 4, 1024)]
        sqE = [V(E0 + 16384, BF16, TCW), V(E0 + 16384 + 256, BF16, TCW)]
        rsE = [V(E0 + 16896, F32, TCW), V(E0 + 16896, F32, TCW)]
        assert E0 + 17408 <= R2

        def fm_norm_x(dst, gname, dkey):
            for tc in range(NTC):
                b = nb()
                for kc in range(8):
                    sq = sqE[kc % 2]
                    act(sq, xT[:, kc, tc * TCW:(tc + 1) * TCW], AF.Square, [('xT', kc, tc)], [('sqE', kc % 2)])
                    mm(bank(b), ones, sq, kc == 0, kc == 7, [('sqE', kc % 2), 'cbf'], [PK(b)])
                rs = rsE[0]
                rstd_from_bank(b, D, rs, 128, ('rsE', 0))
                for kc in range(8):
                    stt('dve', dst[:, kc, tc * TCW:(tc + 1) * TCW], xT[:, kc, tc * TCW:(tc + 1) * TCW], vc(gname, kc), rs, ALU.mult, ALU.mult,
                        [('xT', kc, tc), ('rsE', 0), 'vec'], [(dkey, kc, tc)])
        fm_norm_x(h2T, 'norm_ffn', 'h2T')
        dump('h2T', h2T[:, 7, 1536:2048], 512, [('h2T', 7, 3)])
        w_up_v = w_up_d.rearrange("(kc p) n -> p kc n", p=128)
        rounds = [list(range(i, min(i + 4, NFF))) for i in range(0, NFF, 4)]
        pairs = [(r, jj, j) for r in range(len(rounds)) for jj, j in enumerate(rounds[r])]

        def pre_up(pi):
            r, jj, j = pairs[pi]
            s = pi % 2
            st, wb = wust[s], wubf[s]
            S.dma(lambda e, st=st, j=j: e.dma_start(out=st[:, :, 0:128], in_=w_up_v[:, :, j * 128:(j + 1) * 128]), writes=[('wust', s)])
            S.dma(lambda e, st=st, j=j: e.dma_start(out=st[:, :, 128:256], in_=w_up_v[:, :, DFF + j * 128:DFF + (j + 1) * 128]), writes=[('wust', s)])
            cp('act', wb, st, [('wust', s)], [('wubf', s)])

        dctr = [0]

        def pre_down(r, jj):
            j = rounds[r][jj]
            slot = r % 2
            ds = dctr[0] % 2
            dctr[0] += 1
            st = wdst[ds]
            S.dma(lambda e, st=st, j=j: e.dma_start(out=st, in_=w_down_d[j * 128:(j + 1) * 128, :]), writes=[('wdst', ds)])
            cp('act', wdbf[slot][:, jj, :], st, [('wdst', ds)], [('wdbf', slot, jj)])

        def up_pair(pi):
            r, jj, j = pairs[pi]
            slot = r % 2
            s = pi % 2
            wb = wubf[s]
            for gu in range(2):
                for tc in range(NTC):
                    b = gu * 4 + tc
                    for kc in range(8):
                        mm(bank(b), wb[:, kc, gu * 128:(gu + 1) * 128], h2T[:, kc, tc * TCW:(tc + 1) * TCW], kc == 0, kc == 7,
                           [('wubf', s), ('h2T', kc, tc)], [PK(b)])
                psv = ps[:, gu * 2048:(gu + 1) * 2048]
                pk = [PK(gu * 4 + t_) for t_ in range(4)]
                ci = j if gu == 0 else NFF + j
                acc, akey = (accg, 'accg') if gu == 0 else (accu, 'accu')
                conv3(psv, vc('fcw0', ci), vc('fcw1', ci), vc('fcw2', ci), vc('fcb', ci), acc, akey, pk)
                if gu == 0:
                    act(actb[slot][:, jj, :], accg, AF.Silu, ['accg'], [('actb', slot, jj)])
            tt('pool', actb[slot][:, jj, :], actb[slot][:, jj, :], accu, ALU.mult, [('actb', slot, jj), 'accu'], [('actb', slot, jj)])

        def down_round(r):
            slot = r % 2
            nj = len(rounds[r])
            for n in range(8):
                for tc in range(NTC):
                    b = nb()
                    for jj in range(nj):
                        mm(bank(b), wdbf[slot][:, jj, n * 128:(n + 1) * 128], actb[slot][:, jj, tc * TCW:(tc + 1) * TCW], jj == 0, jj == nj - 1,
                           [('wdbf', slot, jj), ('actb', slot, jj)], [PK(b)])
                    sl = xT[:, n, tc * TCW:(tc + 1) * TCW]
                    tt('dve', sl, sl, bank(b), ALU.add, [PK(b), ('xT', n, tc)], [('xT', n, tc)])

        NR = len(rounds)
        NP = len(pairs)
        pre_up(0)
        pi = 0
        for r in range(NR):
            for jj in range(len(rounds[r])):
                if pi + 1 < NP:
                    pre_up(pi + 1)
                if r >= 1 and jj < len(rounds[r - 1]):
                    pre_down(r - 1, jj)
                up_pair(pi)
                pi += 1
            if r >= 1:
                for jj in range(len(rounds[r]), len(rounds[r - 1])):
                    pre_down(r - 1, jj)
                down_round(r - 1)
        S.phase()
        F0 = BASE
        pTb = V(F0, BF16, 2, T)
        pst = V(F0 + 2048, F32, 2, T)
        wpb = V(F0 + 6144, BF16, 2, 1024)
        wpst = V(F0 + 7168, F32, 2, 1024)
        S.dma(lambda e: e.dma_start(out=pst, in_=pT_d.rearrange("(kc p) t -> p kc t", p=128)), writes=['pst'])
        S.dma(lambda e: e.dma_start(out=wpst, in_=w_ple_d.rearrange("(kc p) n -> p kc n", p=128)), writes=['wpst'])
        for jj in range(len(rounds[NR - 1])):
            pre_down(NR - 1, jj)
        cp('pool', pTb, pst, ['pst'], ['pTb'])
        cp('pool', wpb, wpst, ['wpst'], ['wpb'])
        down_round(NR - 1)
        dump('xffn', xT[:, 5, 0:512], 512, [('xT', 5, 0)])

        S.phase()
        xb = V(YB, BF16, 8, T)
        eT = V(YB + 8192, BF16, 8, T)
        wgst = [V(F0 + 9216, F32, 8, 128), V(F0 + 10240, F32, 8, 128)]
        wgbf = [V(F0 + 11264, BF16, 8, 128), V(F0 + 11776, BF16, 8, 128)]
        sqF = [V(F0 + 12288, BF16, T), V(F0 + 13312, BF16, T)]
        rsF = [V(F0 + 14336, F32, TCW), V(F0 + 14848, F32, TCW)]
        sgt = [V(F0 + 15360, F32, TCW), V(F0 + 15872, F32, TCW)]
        assert F0 + 16384 <= R2
        for kc in range(8):
            cp('dve', xb[:, kc, :], xT[:, kc, :], [('xT', kc, t_) for t_ in range(4)], [('xb', kc)])
        for n in range(8):
            for tc in range(NTC):
                b = nb()
                for kc in range(2):
                    mm(bank(b), wpb[:, kc, n * 128:(n + 1) * 128], pTb[:, kc, tc * TCW:(tc + 1) * TCW], kc == 0, kc == 1, ['wpb', 'pTb'], [PK(b)])
                cp('act', eT[:, n, tc * TCW:(tc + 1) * TCW], bank(b), [PK(b)], [('eT', n)])
        fm_norm_inplace(eT, 8, D, 'ple_norm', 'eT', sqF, rsF, 'sqF', 'rsF')
        dump('eT', eT[:, 4, 0:512], 512, [('eT', 4)])
        w_gate_v = w_gate_d.rearrange("(kc p) n -> p kc n", p=128)
        out_toks = []
        def pre_wg(n):
            st, wb = wgst[n % 2], wgbf[n % 2]
            S.dma(lambda e, st=st, n=n: e.dma_start(out=st, in_=w_gate_v[:, :, n * 128:(n + 1) * 128]), writes=[('wgst', n % 2)])
            cp('pool', wb, st, [('wgst', n % 2)], [('wgbf', n % 2)])

        pre_wg(0)
        for n in range(8):
            wb = wgbf[n % 2]
            if n + 1 < 8:
                pre_wg(n + 1)
            for tc in range(NTC):
                b = nb()
                for kc in range(8):
                    mm(bank(b), wb[:, kc, :], xb[:, kc, tc * TCW:(tc + 1) * TCW], kc == 0, kc == 7, [('wgbf', n % 2), ('xb', kc)], [PK(b)])
                sg = sgt[tc % 2]
                act(sg, bank(b), AF.Sigmoid, [PK(b)], [('sg', tc % 2)])
                tt('dve', sg, sg, eT[:, n, tc * TCW:(tc + 1) * TCW], ALU.mult, [('sg', tc % 2), ('eT', n)], [('sg', tc % 2)])
                sl = xT[:, n, tc * TCW:(tc + 1) * TCW]
                tt('dve', sl, sl, sg, ALU.add, [('sg', tc % 2), ('xT', n, tc)], [('xT', n, tc)])
            out_toks.append(S.dma(lambda e, n=n: e.dma_start(out=outT_d[n * 128:(n + 1) * 128, :], in_=xT[:, n, :]),
                                  reads=[('xT', n, t_) for t_ in range(4)], writes=[('out', n)]))
        S.finish(out_toks + [('dma', n) for n in range(max(0, S.n_dma - N_DMA_SEMS), S.n_dma)])
        S.emit(block, sems, dsems)
    return nc


_CONST_CACHE = {}


def _consts():
    if 'c' in _CONST_CACHE:
        return _CONST_CACHE['c']
    bf = ml_dtypes.bfloat16
    c = {}
    cb = np.zeros((128, 736), np.float32)
    cb[:, 0:128] = np.eye(128)
    cb[:, 128:256] = 1.0
    R = np.zeros((96, 96), np.float32)
    for i in range(16):
        R[80 + i, 64 + i] = -1.0
        R[64 + i, 80 + i] = 1.0
    cb[0:96, 256:352] = R
    E = np.zeros((32, 96), np.float32)
    for i in range(32):
        E[i, 64 + i] = 1.0
    cb[0:32, 352:448] = E
    cb[:, 448] = (-1.0) ** np.arange(128)
    cb[0:96, 480:608] = 1.0
    cb[0:96, 608:704] = R
    c['cbf'] = cb.astype(bf)
    c['altrow'] = ((-1.0) ** np.arange(T)).astype(np.float32).reshape(1, T).astype(bf)
    pos = np.arange(T, dtype=np.float32)
    inv_freq = (10000.0 ** (-np.arange(0, 32, 2, dtype=np.float32) / 32)).astype(np.float32)
    ang = pos[None, :] * inv_freq[:, None]
    rope = np.zeros((2, 96, T), np.float32)
    rope[0, 0:64] = 1.0
    rope[0, 64:80] = np.cos(ang)
    rope[0, 80:96] = np.cos(ang)
    rope[1, 64:80] = np.sin(ang)
    rope[1, 80:96] = np.sin(ang)
    c['rope'] = rope
    L = T
    t = np.linspace(0.0, 1.0, L, dtype=np.float32)
    bands = 16
    fb = np.linspace(1e-4, bands - 1, bands, dtype=np.float32)
    w = (2.0 * np.pi * np.arange(L, dtype=np.float32) / L).astype(np.float32)
    a2 = w[:, None] * fb[None, :]
    z = np.concatenate([t[:, None], np.cos(a2), -np.sin(a2)], axis=-1).astype(np.float32)
    c['zT'] = np.ascontiguousarray(z.T)
    min_decay = np.log(1e-2) / 0.3
    max_decay = np.log(1e-2) / 1.5
    deltas = np.abs(np.linspace(min_decay, max_decay, 512, dtype=np.float32))
    c['decay'] = np.exp(-t[:, None] * deltas[None, :]).astype(np.float32)
    idx = np.arange(T, dtype=np.int64)
    prod = (idx[:, None] * idx[None, :]) % 4096
    angd = prod.astype(np.float64) * (2.0 * np.pi / 4096.0)
    C = np.cos(angd).astype(np.float32)
    Sn = np.sin(angd).astype(np.float32)
    c['cinv'] = C.astype(bf)
    c['sinv'] = Sn.astype(bf)
    c['ctf'] = np.ascontiguousarray(c['cinv'].reshape(16, 128, 16, 128).transpose(2, 1, 0, 3)).reshape(16, 128, 2048)
    c['stf'] = np.ascontiguousarray(c['sinv'].reshape(16, 128, 16, 128).transpose(2, 1, 0, 3)).reshape(16, 128, 2048)
    _CONST_CACHE['c'] = c
    return c


def _colpack(v, rows=128):
    v = np.asarray(v, np.float32).reshape(-1)
    k = (v.size + rows - 1) // rows
    out = np.zeros((rows, k), np.float32)
    for j in range(k):
        seg = v[j * rows:(j + 1) * rows]
        out[:seg.size, j] = seg
    return out


def _pack_vecs(inp):
    cols = {}
    cols['norm_mix'] = _colpack(inp['norm_mix'][0])
    cols['q_norm'] = _colpack(inp['q_norm'][0])
    cols['kv_norm'] = _colpack(inp['kv_norm'][0])
    cols['gq'] = _colpack(inp['qk_norm_q'][0])
    cols['gk'] = _colpack(inp['qk_norm_k'][0])
    for i in range(3):
        cols['scw%d' % i] = _colpack(inp['short_conv_w'][0, i])
        cols['fcw%d' % i] = _colpack(inp['ffn_conv_w'][0, i])
    cols['scb'] = _colpack(inp['short_conv_b'][0])
    cols['fcb'] = _colpack(inp['ffn_conv_b'][0])
    cols['hbias'] = _colpack(inp['hyena_bias'][0])
    cols['g_attn'] = _colpack(inp['out_norm_attn'][0])
    cols['g_hy'] = _colpack(inp['out_norm_hyena'][0])
    cols['norm_ffn'] = _colpack(inp['norm_ffn'][0])
    cols['ple_norm'] = _colpack(inp['ple_norm'][0])
    cols['fb1'] = _colpack(inp['filt_b1'][0])
    cols['fb2'] = _colpack(inp['filt_b2'][0])
    cols['fb3'] = _colpack(inp['filt_b3'][0])
    cols['ffr'] = _colpack(inp['filt_freq'][0])
    cols['eps'] = np.full((128, 1), EPS, np.float32)
    wf0 = np.full((128, 1), 2.0 / 4096, np.float32)
    wf0[0, 0] = 1.0 / 4096
    cols['wf0'] = wf0
    cols['wf1'] = np.full((128, 1), 2.0 / 4096, np.float32)
    out = np.zeros((128, NV), np.float32)
    for name, k in VEC_LAYOUT:
        a = cols[name]
        assert a.shape[1] == k, (name, a.shape, k)
        out[:, VCOL[name]:VCOL[name] + k] = a
    return out


def make_in_maps(inp, ncores=8):
    c = _consts()
    f32 = lambda a: np.ascontiguousarray(np.asarray(a, np.float32))
    shared = {
        'w_in': f32(inp['w_in'][0]), 'w_uq': f32(inp['w_uq'][0]), 'w_ukv': f32(inp['w_ukv'][0]),
        'w_out': f32(inp['w_out'][0]), 'w_up': f32(inp['w_up'][0]), 'w_down': f32(inp['w_down'][0]),
        'w_ple': f32(inp['w_ple'][0]), 'w_gate': f32(inp['w_ple_gate'][0]),
        'fw1': f32(inp['filt_w1'][0]), 'fw2': f32(inp['filt_w2'][0]), 'fw3': f32(inp['filt_w3'][0]),
        'fwo': f32(inp['filt_w_out'][0]), 'vecs': _pack_vecs(inp),
    }
    shared.update(c)
    maps = []
    for b in range(ncores):
        m = dict(shared)
        m['xT'] = np.ascontiguousarray(np.asarray(inp['x'][b], np.float32).T)
        m['pT'] = np.ascontiguousarray(np.asarray(inp['p'][0, b], np.float32).T)
        maps.append(m)
    return maps


_PROG = {}


def kernel(**inputs):
    if 'nc' not in _PROG:
        _PROG['nc'] = build_program()
    nc = _PROG['nc']
    maps = make_in_maps(inputs, 8)
    res = run_bass_kernel_spmd(nc, maps, core_ids=list(range(8)))
    out = np.stack([np.asarray(r['outT'], np.float32).T for r in res.results], axis=0)
    return np.ascontiguousarray(out)
```
